# Optimizing a Trainium2 kernel written in Bass

```python
import math
import jax, jax.numpy as jnp
from jax import lax
import numpy as np

D_MODEL = 1024
BATCH = 8
SEQ = 2048
DEPTH = 1
DEC_BATCH = 128
DEC_SEQ = 8
PAST_LEN = 16384
PAGE_SIZE = 128

D_MIX = D_MODEL
HG_HEADS = 4
HG_DK = 128
HG_DV = D_MIX // 2 // HG_HEADS
ML_HEADS = 4
ML_DK = 128
ML_DV = D_MIX // 2 // ML_HEADS
HG_KW = HG_HEADS * HG_DK
HG_W = HG_HEADS * HG_DV
ML_KW = ML_HEADS * ML_DK
ML_W = ML_HEADS * ML_DV
D_FF = 2816
CONV_W = 3
CHUNK = 64
EPS = 1e-6
IN_SIZES = (HG_KW, HG_KW, HG_W, HG_W, ML_KW, ML_KW, ML_W, ML_W, ML_HEADS, ML_HEADS)
IN_COLS = 2 * HG_KW + 2 * HG_W + 2 * ML_KW + 2 * ML_W + 2 * ML_HEADS

kernel_name = 'hymba_hgrn2_mlstm_convffn_step'


def _split_points():
    pts, acc = [], 0
    for s in IN_SIZES[:-1]:
        acc += s
        pts.append(acc)
    return pts


def _rmsnorm(x, g):
    xf = x.astype(jnp.float32)
    y = xf * lax.rsqrt(jnp.mean(xf * xf, axis=-1, keepdims=True) + EPS)
    return (y * g.astype(jnp.float32)).astype(x.dtype)


def _chunk_len(T):
    return max(d for d in range(1, min(CHUNK, T) + 1) if T % d == 0)


def _to_chunks(a, L):
    B, T, H, d = a.shape
    return a.reshape(B, T // L, L, H, d).transpose(1, 0, 3, 2, 4)


def _from_chunks(a):
    N, B, H, L, d = a.shape
    return a.transpose(1, 0, 3, 2, 4).reshape(B, N * L, H, d)


def _hgrn2_chunked(q, logf, k, v, S0):
    T = q.shape[1]
    L = _chunk_len(T)
    mask = jnp.tril(jnp.ones((L, L), dtype=bool))

    def body(S, inp):
        qc, gc, kc, vc = inp
        b = jnp.cumsum(gc, axis=2)
        inter = jnp.einsum('bhtk,bhkv->bhtv', qc * jnp.exp(b), S)
        diff = b[:, :, :, None, :] - b[:, :, None, :, :]
        decay = jnp.exp(jnp.where(mask[:, :, None], diff, -jnp.inf))
        A = jnp.einsum('bhtk,bhtsk,bhsk->bhts', qc, decay, kc)
        intra = jnp.einsum('bhts,bhsv->bhtv', A, vc)
        b_last = b[:, :, -1]
        S_new = jnp.exp(b_last)[..., None] * S + jnp.einsum(
            'bhsk,bhsv->bhkv', kc * jnp.exp(b_last[:, :, None] - b), vc)
        return S_new, inter + intra

    S_end, o = lax.scan(body, S0, (_to_chunks(q, L), _to_chunks(logf, L),
                                   _to_chunks(k, L), _to_chunks(v, L)))
    return _from_chunks(o), S_end


def _mlstm_chunked(q, k, v, ig, lf, C0, n0, m0):
    T = q.shape[1]
    L = _chunk_len(T)
    mask = jnp.tril(jnp.ones((L, L), dtype=bool))

    def body(carry, inp):
        C, n, m = carry
        qc, kc, vc, ic, fc = inp
        ic = ic[..., 0]
        fc = fc[..., 0]
        a = jnp.cumsum(fc, axis=-1)
        logD = a[..., :, None] - a[..., None, :] + ic[..., None, :]
        logD = jnp.where(mask, logD, -jnp.inf)
        log_inter = a + m[..., None]
        m_t = jnp.maximum(log_inter, jnp.max(logD, axis=-1))
        Dw = jnp.exp(logD - m_t[..., None])
        wi = jnp.exp(log_inter - m_t)
        Sw = jnp.einsum('bhtk,bhsk->bhts', qc, kc) * Dw
        num = wi[..., None] * jnp.einsum('bhtk,bhkv->bhtv', qc, C) + jnp.einsum('bhts,bhsv->bhtv', Sw, vc)
        den = wi * jnp.einsum('bhtk,bhk->bht', qc, n) + jnp.sum(Sw, axis=-1)
        h = num / jnp.maximum(jnp.abs(den), jnp.exp(-m_t))[..., None]
        a_L = a[..., -1]
        log_end = a_L[..., None] - a + ic
        m_new = jnp.maximum(a_L + m, jnp.max(log_end, axis=-1))
        w_end = jnp.exp(log_end - m_new[..., None])
        f_end = jnp.exp(a_L + m - m_new)
        C_new = f_end[..., None, None] * C + jnp.einsum('bhs,bhsk,bhsv->bhkv', w_end, kc, vc)
        n_new = f_end[..., None] * n + jnp.einsum('bhs,bhsk->bhk', w_end, kc)
        return (C_new, n_new, m_new), h

    (C_e, n_e, m_e), h = lax.scan(body, (C0, n0, m0), (
        _to_chunks(q, L), _to_chunks(k, L), _to_chunks(v, L),
        _to_chunks(ig[..., None], L), _to_chunks(lf[..., None], L)))
    return _from_chunks(h), C_e, n_e, m_e


def _layer(x, S_hg, C_ml, n_ml, m_ml, conv_buf, lb, norm1_g, w_in, hg_norm_g, ml_b_ig,
           ml_b_fg, ml_norm_g, w_out, norm2_g, w_gate, w_val, conv_w, conv_b, w_down):
    B, T, _ = x.shape
    f32 = jnp.float32
    h = _rmsnorm(x, norm1_g)
    proj = h @ w_in
    hq, hf, hi, hgate, mq, mk, mv, mo, mig, mfg = jnp.split(proj, _split_points(), axis=-1)

    f = lb + (1.0 - lb) * jax.nn.sigmoid(hf.astype(f32))
    logf = jnp.log(f).reshape(B, T, HG_HEADS, HG_DK)
    k_hg = (1.0 - f).reshape(B, T, HG_HEADS, HG_DK)
    o_hg, S_new = _hgrn2_chunked(hq.astype(f32).reshape(B, T, HG_HEADS, HG_DK), logf, k_hg,
                                 hi.astype(f32).reshape(B, T, HG_HEADS, HG_DV), S_hg.astype(f32))
    o_hg = _rmsnorm(o_hg, hg_norm_g.reshape(HG_HEADS, HG_DV)) * jax.nn.silu(
        hgate.astype(f32).reshape(B, T, HG_HEADS, HG_DV))
    o_hg = o_hg.reshape(B, T, HG_W).astype(x.dtype)

    q_ml = mq.astype(f32).reshape(B, T, ML_HEADS, ML_DK)
    k_ml = mk.astype(f32).reshape(B, T, ML_HEADS, ML_DK) * (ML_DK ** -0.5)
    v_ml = mv.astype(f32).reshape(B, T, ML_HEADS, ML_DV)
    ig = (mig + ml_b_ig).astype(f32)
    lf = jax.nn.log_sigmoid((mfg + ml_b_fg).astype(f32))
    o_ml, C_new, n_new, m_new = _mlstm_chunked(q_ml, k_ml, v_ml, ig, lf, C_ml.astype(f32),
                                               n_ml.astype(f32), m_ml.astype(f32))
    o_ml = _rmsnorm(o_ml, ml_norm_g.reshape(ML_HEADS, ML_DV)) * jax.nn.sigmoid(
        mo.astype(f32).reshape(B, T, ML_HEADS, ML_DV))
    o_ml = o_ml.reshape(B, T, ML_W).astype(x.dtype)

    x = x + jnp.concatenate([o_hg, o_ml], axis=-1) @ w_out

    h2 = _rmsnorm(x, norm2_g)
    u = h2 @ w_gate
    val = h2 @ w_val
    upad = jnp.concatenate([conv_buf.astype(u.dtype), u], axis=1)
    conv = conv_b
    for j in range(CONV_W):
        conv = conv + conv_w[j] * upad[:, j:j + T]
    x = x + (jax.nn.gelu(conv) * val) @ w_down
    new_buf = upad[:, T:]
    dt = x.dtype
    return x, S_new.astype(dt), C_new.astype(dt), n_new.astype(dt), m_new.astype(dt), new_buf.astype(dt)


def _trunk(x, S_hg, C_ml, n_ml, m_ml, conv_buf, norm1_g, w_in, hg_lb_logits, hg_norm_g, ml_b_ig,
           ml_b_fg, ml_norm_g, w_out, norm2_g, w_gate, w_val, conv_w, conv_b, w_down, final_norm_g):
    lb_all = jnp.cumsum(jax.nn.softmax(hg_lb_logits.astype(jnp.float32), axis=0), axis=0)
    Ss, Cs, ns, ms, bufs = [], [], [], [], []
    for l in range(DEPTH):
        x, s, c, n, m, b = _layer(x, S_hg[l], C_ml[l], n_ml[l], m_ml[l], conv_buf[l], lb_all[l],
                                  norm1_g[l], w_in[l], hg_norm_g[l], ml_b_ig[l], ml_b_fg[l],
                                  ml_norm_g[l], w_out[l], norm2_g[l], w_gate[l], w_val[l],
                                  conv_w[l], conv_b[l], w_down[l])
        Ss.append(s); Cs.append(c); ns.append(n); ms.append(m); bufs.append(b)
    y = _rmsnorm(x, final_norm_g)
    return y, jnp.stack(Ss), jnp.stack(Cs), jnp.stack(ns), jnp.stack(ms), jnp.stack(bufs)


def setup_inputs(seed: int = 0) -> dict:
    key = jax.random.key(seed)
    ks = jax.random.split(key, 24)
    nrm = jax.random.normal
    f32 = jnp.float32
    return {
        'x_prompt': nrm(ks[0], (BATCH, SEQ, D_MODEL), f32),
        'x_sample': nrm(ks[1], (DEC_BATCH, DEC_SEQ, D_MODEL), f32),
        'state_hgrn_S': 0.5 * nrm(ks[2], (DEPTH, DEC_BATCH, HG_HEADS, HG_DK, HG_DV), f32),
        'state_mlstm_C': 0.5 * nrm(ks[3], (DEPTH, DEC_BATCH, ML_HEADS, ML_DK, ML_DV), f32),
        'state_mlstm_n': 0.5 * nrm(ks[4], (DEPTH, DEC_BATCH, ML_HEADS, ML_DK), f32),
        'state_mlstm_m': nrm(ks[5], (DEPTH, DEC_BATCH, ML_HEADS), f32),
        'state_conv': nrm(ks[6], (DEPTH, DEC_BATCH, CONV_W - 1, D_FF), f32),
        'norm1_g': 1.0 + 0.02 * nrm(ks[7], (DEPTH, D_MODEL), f32),
        'w_in': nrm(ks[8], (DEPTH, D_MODEL, IN_COLS), f32) * D_MODEL ** -0.5,
        'hg_lb_logits': 0.1 * nrm(ks[9], (DEPTH + 1, HG_KW), f32),
        'hg_norm_g': 1.0 + 0.02 * nrm(ks[10], (DEPTH, HG_W), f32),
        'ml_b_ig': 0.1 * nrm(ks[11], (DEPTH, ML_HEADS), f32),
        'ml_b_fg': 3.0 + 0.1 * nrm(ks[12], (DEPTH, ML_HEADS), f32),
        'ml_norm_g': 1.0 + 0.02 * nrm(ks[13], (DEPTH, ML_W), f32),
        'w_out': nrm(ks[14], (DEPTH, D_MIX, D_MODEL), f32) * D_MIX ** -0.5,
        'norm2_g': 1.0 + 0.02 * nrm(ks[15], (DEPTH, D_MODEL), f32),
        'w_gate': nrm(ks[16], (DEPTH, D_MODEL, D_FF), f32) * D_MODEL ** -0.5,
        'w_val': nrm(ks[17], (DEPTH, D_MODEL, D_FF), f32) * D_MODEL ** -0.5,
        'conv_w': nrm(ks[18], (DEPTH, CONV_W, D_FF), f32) * CONV_W ** -0.5,
        'conv_b': 0.02 * nrm(ks[19], (DEPTH, D_FF), f32),
        'w_down': nrm(ks[20], (DEPTH, D_FF, D_MODEL), f32) * D_FF ** -0.5,
        'final_norm_g': 1.0 + 0.02 * nrm(ks[21], (D_MODEL,), f32),
    }


def reference(x_prompt, x_sample, state_hgrn_S, state_mlstm_C, state_mlstm_n, state_mlstm_m,
              state_conv, norm1_g, w_in, hg_lb_logits, hg_norm_g, ml_b_ig, ml_b_fg, ml_norm_g,
              w_out, norm2_g, w_gate, w_val, conv_w, conv_b, w_down, final_norm_g):
    B = x_prompt.shape[0]
    dt = x_prompt.dtype
    S0 = jnp.zeros((DEPTH, B, HG_HEADS, HG_DK, HG_DV), dt)
    C0 = jnp.zeros((DEPTH, B, ML_HEADS, ML_DK, ML_DV), dt)
    n0 = jnp.zeros((DEPTH, B, ML_HEADS, ML_DK), dt)
    m0 = jnp.zeros((DEPTH, B, ML_HEADS), dt)
    buf0 = jnp.zeros((DEPTH, B, CONV_W - 1, D_FF), dt)
    y_prompt, S_p, C_p, n_p, m_p, buf_p = _trunk(
        x_prompt, S0, C0, n0, m0, buf0, norm1_g, w_in, hg_lb_logits, hg_norm_g, ml_b_ig, ml_b_fg,
        ml_norm_g, w_out, norm2_g, w_gate, w_val, conv_w, conv_b, w_down, final_norm_g)
    y_sample, S_s, C_s, n_s, m_s, buf_s = _trunk(
        x_sample, state_hgrn_S, state_mlstm_C, state_mlstm_n, state_mlstm_m, state_conv,
        norm1_g, w_in, hg_lb_logits, hg_norm_g, ml_b_ig, ml_b_fg, ml_norm_g, w_out, norm2_g,
        w_gate, w_val, conv_w, conv_b, w_down, final_norm_g)
    return (y_prompt, y_sample, S_p, S_s, C_p, C_s, n_p, n_s, m_p, m_s, buf_p, buf_s)
```

```python
import contextlib
import math
import numpy as np
import ml_dtypes
import concourse.bass as bass
import concourse.mybir as mybir
from concourse.bass_utils import run_bass_kernel_spmd

F32 = mybir.dt.float32
BF16 = mybir.dt.bfloat16
AF = mybir.ActivationFunctionType
ALU = mybir.AluOpType

NCORES = 8
P = 128
D = 1024
NKC = 8
IN_COLS = 4104
DFF = 2816
NFC = 22
SEQ = 2048
NPT = 16
SAMPLE = 16
DEC_PER_CORE = 16
DEC_SEQ = 8
EPS = 1e-6
LNS = -0.5 * math.log(128.0)
BT = 2
NTB = BT * P
SBS = [[0, 1, 2, 3, 4, 5, 6, 7, SAMPLE], [8, 9, 10, 11, 12, 13, 14, 15]]
NSLOT = 9
GFF = 2
ENGS = ("pe", "act", "dve", "pool", "sp")


class Op:
    __slots__ = ("eng", "fn", "deps", "idx", "is_dma", "sig", "semval", "sem", "prev", "tag")

    def __init__(self, eng, fn, is_dma):
        self.tag = ""
        self.eng = eng
        self.fn = fn
        self.deps = []
        self.is_dma = is_dma
        self.sig = False
        self.semval = None
        self.sem = None
        self.prev = 0


class Res:
    __slots__ = ("name", "writer", "readers")

    def __init__(self, name=""):
        self.name = name
        self.writer = None
        self.readers = {}


class Sched:
    def __init__(self, nc):
        self.nc = nc
        self.streams = {e: [] for e in ENGS}
        self.n_dma_sems = {"sp": 48, "pool": 28, "act": 8}
        self.out_dmas = []
        self.fence_deps = []
        self.dmas_since_fence = []
        self.tag = ""
        self.fence_scratch = None

    def add(self, eng, fn, reads=(), writes=(), dma=False, extra=()):
        op = Op(eng, fn, dma)
        op.tag = self.tag
        op.idx = len(self.streams[eng])
        deps = list(self.fence_deps)
        for r in reads:
            if r.writer is not None:
                deps.append(r.writer)
        for w in writes:
            if w.writer is not None:
                deps.append(w.writer)
            deps.extend(w.readers.values())
        deps.extend(extra)
        seen = set()
        for d in deps:
            if d is op or id(d) in seen:
                continue
            seen.add(id(d))
            op.deps.append(d)
        key = ("dma", id(op)) if dma else eng
        for r in reads:
            r.readers[key] = op
        for w in writes:
            w.writer = op
            w.readers = {}
        self.streams[eng].append(op)
        if dma:
            self.dmas_since_fence.append(op)
        return op

    def fence(self):
        deps = []
        for e in ENGS:
            for op in reversed(self.streams[e]):
                if not op.is_dma:
                    deps.append(op)
                    break
        deps.extend(self.dmas_since_fence)
        self.dmas_since_fence = []
        if self.fence_scratch is not None:
            scr = self.fence_scratch
            self.fence_deps = []
            op = self.add("pool", lambda e: e.memset(scr, 0.0), extra=deps)
            self.fence_deps = [op]
        else:
            self.fence_deps = deps

    @staticmethod
    def _near(op, d):
        return op.idx - d.idx <= 4

    def emit(self):
        nc = self.nc
        for e in ENGS:
            for op in self.streams[e]:
                for d in op.deps:
                    if d.is_dma or d.eng != op.eng:
                        d.sig = True
                    elif e != "pe" and self._near(op, d):
                        d.sig = True
        with contextlib.ExitStack() as st:
            csem = {e: st.enter_context(nc.semaphore("s_" + e)) for e in ENGS}
            dsems = {e: [st.enter_context(nc.semaphore("d_%s%d" % (e, i))) for i in range(n)]
                     for e, n in self.n_dma_sems.items()}
            for e in ENGS:
                cnt = 0
                dcnt = 0
                dvals = [0] * self.n_dma_sems.get(e, 1)
                for op in self.streams[e]:
                    if op.is_dma:
                        k = dcnt % self.n_dma_sems[e]
                        dcnt += 1
                        op.sem = dsems[e][k]
                        op.prev = dvals[k]
                        dvals[k] += 16
                        op.semval = dvals[k]
                    elif op.sig:
                        cnt += 1
                        op.sem = csem[e]
                        op.semval = cnt
            block = st.enter_context(nc.Block())

            def run(e):
                def body(eng):
                    waited = {}
                    for op in self.streams[e]:
                        needs = {}
                        for d in op.deps:
                            if d.semval is None:
                                continue
                            if (not d.is_dma) and d.eng == e and (e == "pe" or not self._near(op, d)):
                                continue
                            k = id(d.sem)
                            if needs.get(k, (None, 0))[1] < d.semval:
                                needs[k] = (d.sem, d.semval)
                        if op.is_dma and op.prev > 0:
                            k = id(op.sem)
                            if needs.get(k, (None, 0))[1] < op.prev:
                                needs[k] = (op.sem, op.prev)
                        for k, (sem, val) in needs.items():
                            if waited.get(k, 0) >= val:
                                continue
                            eng.wait_ge(sem, val)
                            waited[k] = val
                        ins = op.fn(eng)
                        if op.is_dma:
                            ins.then_inc(op.sem, 16)
                        elif op.sig:
                            ins.then_inc(op.sem, 1)
                    if e == "sp":
                        for d in self.out_dmas:
                            k = id(d.sem)
                            if waited.get(k, 0) < d.semval:
                                eng.wait_ge(d.sem, d.semval)
                                waited[k] = d.semval
                return body

            block.tensor(run("pe"))
            block.scalar(run("act"))
            block.vector(run("dve"))
            block.gpsimd(run("pool"))
            block.sync(run("sp"))
        import os
        if os.environ.get("KDUMP"):
            import json
            dump = {e: [{"tag": op.tag, "dma": op.is_dma, "deps": [[d.eng, d.idx, d.is_dma] for d in op.deps]} for op in self.streams[e]]
                    for e in ENGS}
            json.dump(dump, open(os.environ["KDUMP"], "w"))


class Buf:
    __slots__ = ("ap", "res")

    def __init__(self, ap, name=""):
        self.ap = ap
        self.res = Res(name)


class Ring:
    def __init__(self, bufs):
        self.bufs = bufs
        self.i = 0

    def get(self):
        b = self.bufs[self.i % len(self.bufs)]
        self.i += 1
        return b


CF_IDENT = 0
CF_MASK64 = 128
CF_MASK8 = 256
CF_RM64 = 384
CF_RM8 = CF_RM64 + NTB
CF_ROWMASK = CF_RM8 + 128
CF_RNEG8 = CF_ROWMASK + 16
CF_NHALF = CF_RNEG8 + 128
CF_ONE = CF_NHALF + 8
CF_ZERO = CF_ONE + 1
CF_LNS = CF_ZERO + 1
CF_EPS = CF_LNS + 1
CF_ONEROW = CF_EPS + 5
NCF = CF_ONEROW + 128


def _consts():
    cf = np.zeros((P, NCF), np.float32)
    cf[:, CF_IDENT:CF_IDENT + 128] = np.eye(128, dtype=np.float32)
    s = np.arange(128)[:, None]
    t = np.arange(128)[None, :]
    cf[:, CF_MASK64:CF_MASK64 + 128] = ((s // 64 == t // 64) & (s <= t)).astype(np.float32)
    cf[:, CF_MASK8:CF_MASK8 + 128] = ((s // 8 == t // 8) & (s <= t)).astype(np.float32)
    tt = np.arange(NTB)[None, :]
    cf[:, CF_RM64:CF_RM64 + NTB] = (tt % 64 != 0).astype(np.float32)
    cf[:, CF_RM8:CF_RM8 + 128] = (t % 8 != 0).astype(np.float32)
    cf[:, CF_ROWMASK:CF_ROWMASK + 16] = (s // 8 == np.arange(16)[None, :]).astype(np.float32)
    cf[:, CF_ONE] = 1.0
    cf[:, CF_LNS] = LNS
    cf[:, CF_EPS] = EPS
    cf[:, CF_ONEROW:CF_ONEROW + 128] = 1.0
    cf[:, CF_RNEG8:CF_RNEG8 + 128] = np.where(t % 8 == 0, -1e30, 0.0).astype(np.float32)
    cf[:, CF_NHALF:CF_NHALF + 8] = -0.5
    cb = np.concatenate([np.eye(128, dtype=np.float32), cf[:, CF_MASK64:CF_MASK64 + 128]], axis=1)
    return cf, cb.astype(ml_dtypes.bfloat16)


def build():
    nc = bass.Bass("TRN2", target_bir_lowering=False)

    def din(name, shape, dt=F32):
        return nc.dram_tensor(name, list(shape), dt, kind="ExternalInput").ap()

    def dout(name, shape):
        return nc.dram_tensor(name, list(shape), F32, kind="ExternalOutput").ap()

    xp = din("xp", [SEQ, D])
    xs = din("xs", [P, D])
    sS = din("sS", [16, 4, 128, 128])
    sC = din("sC", [16, 4, 128, 128])
    sn = din("sn", [64, 128])
    sm = din("sm", [4, 16])
    sconv = din("sconv", [32, DFF])
    w_in = din("w_in", [D, IN_COLS])
    w_out = din("w_out", [D, D])
    w_gate = din("w_gate", [D, DFF])
    w_val = din("w_val", [D, DFF])
    w_down = din("w_down", [DFF, D])
    g1 = din("g1", [1, D])
    g2 = din("g2", [1, D])
    gf = din("gf", [1, D])
    lbl = din("lbl", [P, 8])
    gcat_d = din("gcat", [P, 8])
    bigfg = din("bigfg", [4, 2])
    cw = din("cw", [P, NFC * 3])
    cbias = din("cbias", [P, NFC])
    cf_d = din("cf", [P, NCF])
    cb_d = din("cb", [P, 256], BF16)

    y_p = dout("y_p", [SEQ, D])
    y_s = dout("y_s", [P, D])
    oS_p = dout("oS_p", [4, 128, 128])
    oS_s = dout("oS_s", [16, 4, 128, 128])
    oC_p = dout("oC_p", [4, 128, 128])
    oC_s = dout("oC_s", [16, 4, 128, 128])
    on_p = dout("on_p", [4, 128])
    on_s = dout("on_s", [64, 128])
    om_p = dout("om_p", [4, 1])
    om_s = dout("om_s", [4, 16])
    ocv_p = dout("ocv_p", [2, DFF])
    ocv_s = dout("ocv_s", [32, DFF])

    S = Sched(nc)
    st = contextlib.ExitStack()
    with st:
        def sb(name, shape, dt=F32):
            t = st.enter_context(nc.sbuf_tensor(name, list(shape), dt))
            return t

        WIN = sb("WIN", [P, NKC, IN_COLS], BF16)
        WIN_res = [Res("win%d" % g) for g in range(9)]
        WOUT = sb("WOUT", [P, NKC, D], BF16)
        WOUT_res = [Res("wout%d" % k) for k in range(NKC)]
        X2 = sb("X2", [P, NSLOT, D], F32)
        X2b = [Buf(X2[:, i, :], "x2_%d" % i) for i in range(NSLOT)]
        CF = sb("CF", [P, NCF], F32)
        CFr = Res("cf")
        CB = sb("CB", [P, 256], BF16)
        CBr = Res("cb")
        GB = Buf(sb("GB", [P, D])[:, :], "gb")
        HALO = Buf(sb("HALO", [P, NFC, 2])[:, :, :], "halo")
        NROW = Buf(sb("NROW", [4, 128])[:, :], "nrow")
        SMALL = sb("SMALL", [P, 256], F32)
        SMr = Res("small")
        LB = SMALL[:, 0:4]
        OML = SMALL[:, 4:8]
        GCAT = SMALL[:, 8:16]
        LBL = SMALL[:, 16:24]
        BIGFG = SMALL[0:4, 24:26]
        NBFG = SMALL[0:4, 26:27]
        CBIAS = SMALL[:, 32:32 + NFC]
        CW = SMALL[:, 64:64 + 3 * NFC].rearrange("p (c j) -> p c j", j=3)
        SF = Buf(sb("SF", [P, 4, 128])[:, :, :], "SF")
        SBF = [Buf(sb("SBF%d" % i, [P, 4, 128], BF16)[:, :, :], "SBF%d" % i) for i in range(2)]
        CFs = Buf(sb("CFs", [P, 4, 128])[:, :, :], "CFs")
        CBFs = [Buf(sb("CBF%d" % i, [P, 4, 128], BF16)[:, :, :], "CBF%d" % i) for i in range(2)]
        NF = Buf(sb("NF", [P, 4])[:, :], "NF")
        NBF = [Buf(sb("NBF%d" % i, [P, 4], BF16)[:, :], "NBF%d" % i) for i in range(2)]
        CARRY = sb("CARRY", [4, 8], F32)
        FCAR = Buf(CARRY[:, 0:1], "fcar")
        MCAR = Buf(CARRY[:, 1:2], "mcar")
        MOUT = Buf(CARRY[:, 2:3], "mout")
        S.fence_scratch = CARRY[:, 7:8]
        ONEC = Buf(sb("ONEC", [P, 2], BF16)[:, :], "onec")

        PS = [st.enter_context(nc.psum_tensor("ps%d" % i, [P, 512], F32)) for i in range(8)]
        PSr = [Res("ps%d" % i) for i in range(8)]

        WORK_BYTES = 70912 + 4096
        WORK = sb("WORK", [P, WORK_BYTES // 4], F32)

        class Carver:
            def __init__(self):
                self.off = 0

            def take(self, shape_free, dt, name, parts=P):
                n = int(np.prod(shape_free))
                nbytes = n * (2 if dt == BF16 else 4)
                nbytes_al = (nbytes + 31) // 32 * 32
                assert self.off + nbytes_al <= WORK_BYTES, (name, self.off, nbytes_al, WORK_BYTES)
                w0 = self.off // 4
                ap = WORK[0:parts, w0:w0 + nbytes_al // 4]
                if dt == BF16:
                    ap = ap.bitcast(BF16)[:, 0:n]
                else:
                    ap = ap[:, 0:n]
                if len(shape_free) == 2:
                    ap = ap.rearrange("p (a b) -> p a b", b=shape_free[1])
                elif len(shape_free) == 3:
                    ap = ap.rearrange("p (a b c) -> p a b c", b=shape_free[1], c=shape_free[2])
                self.off += nbytes_al
                return Buf(ap, name)

        def dma(eng, out, in_, reads=(), writes=(), **kw):
            return S.add(eng, lambda e: e.dma_start(out=out, in_=in_, **kw), reads=reads, writes=writes, dma=True)

        def mm(out, lhsT, rhs, start, stop, reads, writes, **kw):
            kw.setdefault("skip_group_check", True)
            op = S.add("pe", lambda e: e.matmul(out, lhsT=lhsT, rhs=rhs, start=start, stop=stop, **kw),
                       reads=reads, writes=writes)
            op.tag = op.tag + "|%d*%d*%d" % (lhsT.shape[0], int(np.prod(lhsT.shape[1:])), int(np.prod(rhs.shape[1:])))
            return op

        def tr(out, in_, ident, reads, writes):
            op = S.add("pe", lambda e: e.transpose(out, in_, ident), reads=reads, writes=writes)
            op.tag = op.tag + "|%d*%d*%d" % (in_.shape[0], int(np.prod(in_.shape[1:])), in_.shape[0])
            return op

        def act(out, in_, func, reads, writes, **kw):
            return S.add("act", lambda e: e.activation(out=out, in_=in_, func=func, **kw), reads=reads, writes=writes)

        def ts(eng, out, in0, s1, s2, op0, op1, reads, writes):
            if s2 is None and eng == "pool" and op0 == ALU.mult:
                s2, op1 = 1.0, ALU.mult
            if s2 is None:
                return S.add(eng, lambda e: e.tensor_scalar(out=out, in0=in0, scalar1=s1, scalar2=None, op0=op0),
                             reads=reads, writes=writes)
            return S.add(eng, lambda e: e.tensor_scalar(out=out, in0=in0, scalar1=s1, scalar2=s2, op0=op0, op1=op1),
                         reads=reads, writes=writes)

        def tt(eng, out, in0, in1, op, reads, writes):
            return S.add(eng, lambda e: e.tensor_tensor(out=out, in0=in0, in1=in1, op=op), reads=reads, writes=writes)

        def stt(out, in0, scalar, in1, op0, op1, reads, writes):
            return S.add("dve", lambda e: e.scalar_tensor_tensor(out=out, in0=in0, scalar=scalar, in1=in1, op0=op0, op1=op1),
                         reads=reads, writes=writes)

        def cp(eng, out, in_, reads, writes):
            if eng == "act":
                return act(out, in_, AF.Copy, reads, writes)
            return S.add(eng, lambda e: e.tensor_copy(out=out, in_=in_), reads=reads, writes=writes)

        def scan(out, d0, d1, init, op0, op1, reads, writes):
            return S.add("dve", lambda e: e.tensor_tensor_scan(out=out, data0=d0, data1=d1, initial=init, op0=op0, op1=op1),
                         reads=reads, writes=writes)

        def treduce(out, in_, reads, writes):
            return S.add("dve", lambda e: e.tensor_reduce(out=out, in_=in_, axis=mybir.AxisListType.X, op=ALU.add),
                         reads=reads, writes=writes)

        def memset(eng, ap, val, writes):
            return S.add(eng, lambda e: e.memset(ap, val), writes=writes)

        IDF = CF[:, CF_IDENT:CF_IDENT + 128]
        IDB = CB[:, 0:128]
        MASKB = CB[:, 128:256]

        def psb(i, dt=F32):
            if dt == BF16:
                return PS[i][:, :].bitcast(BF16)
            return PS[i][:, :]

        dma("sp", CF[:, :], cf_d[:, :], writes=[CFr])
        dma("sp", CB[:, :], cb_d[:, :], writes=[CBr])
        dma("sp", LBL, lbl[:, :], writes=[SMr])
        dma("sp", GCAT, gcat_d[:, :], writes=[SMr])
        dma("sp", BIGFG, bigfg[:, :], writes=[SMr])
        dma("sp", CBIAS, cbias[:, :], writes=[SMr])
        dma("sp", SMALL[:, 64:64 + 3 * NFC], cw[:, :], writes=[SMr])
        w_in_r = w_in.rearrange("(kc p) f -> p kc f", p=P)
        for gs in ((0, 1), (8,), (4, 5), (2, 3), (6, 7)):
            c0, c1 = (gs[0] * 512, (gs[-1] + 1) * 512) if gs[0] < 8 else (4096, 4104)
            for g in gs[1:]:
                WIN_res[g] = WIN_res[gs[0]]
            dma("pool", WIN[:, :, c0:c1], w_in_r[:, :, c0:c1], writes=[WIN_res[gs[0]]])
        tt("dve", LB, LBL[:, 0:4], LBL[:, 4:8], ALU.subtract, reads=[SMr], writes=[SMr])
        act(LB, LB, AF.Sigmoid, reads=[SMr], writes=[SMr])
        ts("dve", OML, LB, -1.0, 1.0, ALU.mult, ALU.add, reads=[SMr], writes=[SMr])
        ts("dve", NBFG, BIGFG[:, 1:2], -1.0, None, ALU.mult, None, reads=[SMr], writes=[SMr])
        memset("dve", ONEC.ap, 1.0, writes=[ONEC.res])
        memset("pool", SF.ap, 0.0, writes=[SF.res])
        memset("pool", SBF[0].ap, 0.0, writes=[SBF[0].res])
        memset("pool", CFs.ap, 0.0, writes=[CFs.res])
        memset("pool", CBFs[0].ap, 0.0, writes=[CBFs[0].res])
        memset("pool", NF.ap, 0.0, writes=[NF.res])
        memset("pool", NBF[0].ap, 0.0, writes=[NBF[0].res])
        memset("pool", CARRY[:, :], 0.0, writes=[FCAR.res, MCAR.res, MOUT.res])

        out_dmas = []

        def odma(out, in_, reads, **kw):
            op = dma("sp", out, in_, reads=reads, **kw)
            out_dmas.append(op)
            return op

        def phase1(tiles, slots, first):
            cv = Carver()
            HBF = Ring([cv.take([D], BF16, "hbf%d" % i) for i in range(2)])
            JUNK = cv.take([128], BF16, "junk")
            TF = Ring([cv.take([NTB], F32, "tf%d" % i) for i in range(4)])
            RSB = [cv.take([NTB], F32, "rs%d" % i, parts=4) for i in range(6)]
            RSM = [cv.take([16], F32, "rsm%d" % i, parts=4) for i in range(3)] + [cv.take([64], F32, "rsm3", parts=4)]
            VHG = Ring([cv.take([512], BF16, "vhg%d" % i) for i in range(2)])
            SGT = Ring([cv.take([512], BF16, "sgt%d" % i) for i in range(2)])
            KPP = Ring([cv.take([4, 128], BF16, "kpp%d" % i) for i in range(2)])
            VML = Ring([cv.take([512], BF16, "vml%d" % i) for i in range(2)])
            SO = Ring([cv.take([512], BF16, "so%d" % i) for i in range(2)])
            TSG = Ring([cv.take([512], F32, "tsg%d" % i) for i in range(1)])
            ATB = Ring([cv.take([4, 128], BF16, "atb%d" % i) for i in range(2)])
            KHTOK = cv.take([4, 128], BF16, "khtok")
            OCAT = cv.take([D], BF16, "ocat")
            OT = cv.take([NKC, 128], BF16, "oT")
            SSQ = Ring([cv.take([8], F32, "ssq%d" % i) for i in range(6)])
            XSS = Ring([cv.take([2], F32, "xss%d" % i) for i in range(4)])
            EBR = Ring([cv.take([NTB], F32, "eb%d" % i) for i in range(2)])
            EBs = {}
            EV = Ring([cv.take([NTB], F32, "ev%d" % i) for i in range(2)])
            QSR = Ring([cv.take([NTB], F32, "qs%d" % i) for i in range(2)])

            SM0 = cv.take([16], F32, "sm0", parts=4)

            def make_set(tag):
                d = {}
                d["HT"] = cv.take([NKC, NTB], BF16, "hT" + tag)
                d["HTr"] = [[Res("hT%s_%d_%d" % (tag, t, pk)) for pk in range(2)] for t in range(BT)]
                for nm in ("QT", "KT", "KH", "MQ", "MK"):
                    d[nm] = cv.take([4, NTB], BF16, nm + tag)
                    d[nm + "r"] = [Res("%s%s%d" % (nm, tag, h)) for h in range(4)]
                d["ELAST"] = cv.take([4, 16], F32, "elast" + tag)
                d["ELr"] = [Res("el%s%d" % (tag, h)) for h in range(4)]
                d["TSC"] = [cv.take([12], F32, "tsc%s%d" % (tag, i)) for i in range(BT)]
                d["FEB"] = cv.take([4, 16], F32, "feb" + tag)
                return d
            sets = [make_set("a")]
            mark = cv.off
            SSF = Ring([cv.take([4, 128], F32, "ssf%d" % i) for i in range(3)])
            SSB = Ring([cv.take([4, 128], BF16, "ssb%d" % i) for i in range(3)])
            SSN = Ring([cv.take([4, 128], F32, "ssn%d" % i) for i in range(2)])
            QM = Ring([cv.take([4, 128], BF16, "qm%d" % i) for i in range(2)])
            KM = Ring([cv.take([4, 128], BF16, "km%d" % i) for i in range(2)])
            SN_ROW = cv.take([128], F32, "snrow")
            SN_F = cv.take([64], F32, "snf")
            SN_B = cv.take([64], BF16, "snb")
            SN_NEW = cv.take([64], F32, "snnew")
            SN_QN = cv.take([64], F32, "snqn")
            end_s = cv.off
            cv.off = mark
            sets.append(make_set("b"))
            cv.off = max(cv.off, end_s)

            dma("sp", GB.ap, g1[0, :].partition_broadcast(P), writes=[GB.res])
            def load_wout():
                for kc in range(NKC):
                    wb = X2b[2 + (kc % 7)]
                    dma("act", wb.ap, w_out[kc * P:(kc + 1) * P, :], writes=[wb.res])
                    ts("pool", WOUT[:, kc, :], wb.ap, GCAT[:, kc:kc + 1], None, ALU.mult, None,
                       reads=[wb.res, SMr], writes=[WOUT_res[kc]])

            PBIG = Ring([0, 1, 2])
            PTR = 3
            PA = Ring([4, 5])
            PO_H, PO_M = 6, 7

            blocks = []
            i = 0
            while i < len(tiles):
                if tiles[i] == SAMPLE:
                    blocks.append([i])
                    i += 1
                else:
                    blocks.append(list(range(i, min(i + BT, len(tiles)))))
                    i += len(blocks[-1])

            def lab(gen, label):
                stepc = [0]
                while True:
                    S.tag = "%s.%d" % (label, stepc[0])
                    try:
                        next(gen)
                    except StopIteration:
                        return
                    stepc[0] += 1
                    yield

            def blk_params(blk):
                is_s = tiles[blk[0]] == SAMPLE
                nt = len(blk)
                NT = nt * P
                L = 8 if is_s else 64
                return is_s, nt, NT, L, NT // L

            HBS = {}

            def gen_Ac(blk, bs, key):
                is_s, nt, NT, L, NBb = blk_params(blk)
                for bi, ti in enumerate(blk):
                    tile = tiles[ti]
                    xb = X2b[slots[ti]]
                    src = xs[:, :] if is_s else xp[tile * P:(tile + 1) * P, :]
                    dma("sp", xb.ap, src, writes=[xb.res])
                    ss = XSS.get()
                    hb = HBF.get()
                    act(hb.ap, xb.ap, AF.Square, reads=[xb.res], writes=[hb.res, ss.res], accum_out=ss.ap[:, 0:1])
                    ts("dve", ss.ap[:, 1:2], ss.ap[:, 0:1], 1.0 / D, EPS, ALU.mult, ALU.add, reads=[ss.res], writes=[ss.res])
                    tt("pool", ss.ap[:, 1:2], ss.ap[:, 1:2], CF[:, CF_NHALF:CF_NHALF + 1], ALU.pow, reads=[ss.res, CFr], writes=[ss.res])
                    stt(hb.ap, xb.ap, ss.ap[:, 1:2], GB.ap, ALU.mult, ALU.mult, reads=[xb.res, ss.res, GB.res], writes=[hb.res])
                    HBS[(key, bi)] = hb
                    yield

            def gen_At(blk, bs, key):
                hT = bs["HT"]
                for bi, ti in enumerate(blk):
                    hb = HBS.pop((key, bi))
                    for pk in range(2):
                        tb_ = PBIG.get()
                        pt = psb(tb_, BF16)[:, 0:512].rearrange("p (k t) -> p k t", t=128)
                        for k4 in range(4):
                            kc = pk * 4 + k4
                            tr(pt[:, k4, :], hb.ap[:, kc * P:(kc + 1) * P], IDB, reads=[hb.res, CBr], writes=[PSr[tb_]])
                        cp("act", hT.ap[:, pk * 4:(pk + 1) * 4, bi * P:(bi + 1) * P], pt, reads=[PSr[tb_]], writes=[bs["HTr"][bi][pk]])
                        yield

            def gen_B1(blk, bs):
                is_s, nt, NT, L, NBb = blk_params(blk)
                hT = bs["HT"]
                hTr = [r for bi in range(nt) for r in bs["HTr"][bi]]
                RM = CF[:, CF_RM8:CF_RM8 + 128] if is_s else CF[:, CF_RM64:CF_RM64 + NT]
                QT, KT, KH, MQ, MK = bs["QT"], bs["KT"], bs["KH"], bs["MQ"], bs["MK"]
                QTr, KTr, KHr, MQr, MKr = bs["QTr"], bs["KTr"], bs["KHr"], bs["MQr"], bs["MKr"]
                ELAST, ELr, TSC, FEB = bs["ELAST"], bs["ELr"], bs["TSC"], bs["FEB"]

                def fm_proj(col0, ncols, bank, wres):
                    out = PS[bank][0:ncols, 0:NT]
                    for kc in range(NKC):
                        mm(out, WIN[:, kc, col0:col0 + ncols], hT.ap[:, kc, 0:NT], kc == 0, kc == NKC - 1,
                           reads=[wres] + hTr, writes=[PSr[bank]])
                    return out

                bk_i = PBIG.get()
                p_ig = fm_proj(4096, 4, bk_i, WIN_res[8])
                R_ig = RSB[0]
                ts("dve", R_ig.ap[:, 0:NT], p_ig, BIGFG[:, 0:1], None, ALU.add, None, reads=[PSr[bk_i], SMr], writes=[R_ig.res])
                bk_f = PBIG.get()
                p_fg = fm_proj(4100, 4, bk_f, WIN_res[8])
                R_e = RSB[1]
                act(R_e.ap[:, 0:NT], p_fg, AF.Exp, reads=[PSr[bk_f], SMr], writes=[R_e.res], scale=-1.0, bias=NBFG)
                act(R_e.ap[:, 0:NT], R_e.ap[:, 0:NT], AF.Ln, reads=[R_e.res, CFr], writes=[R_e.res], bias=CF[0:4, CF_ONE:CF_ONE + 1])
                R_F = RSB[2]
                if is_s:
                    scan(R_F.ap[:, 0:NT], CF[0:4, CF_RM8:CF_RM8 + 128], R_e.ap[:, 0:NT], 0.0, ALU.mult, ALU.subtract,
                         reads=[CFr, R_e.res], writes=[R_F.res])
                else:
                    scan(R_F.ap[:, 0:NT], CF[0:4, CF_ONE:CF_ONE + 1].to_broadcast([4, NT]), R_e.ap[:, 0:NT], 0.0, ALU.mult, ALU.subtract,
                         reads=[CFr, R_e.res], writes=[R_F.res])
                R_G = RSB[3]
                tt("dve", R_G.ap[:, 0:NT], R_ig.ap[:, 0:NT], R_F.ap[:, 0:NT], ALU.subtract, reads=[R_ig.res, R_F.res], writes=[R_G.res])
                R_M = RSB[4]
                R_Mc = RSM[0]
                if is_s:
                    dma("sp", SM0.ap, sm[:, :], writes=[SM0.res])
                    R_G2 = RSB[5]
                    cp("dve", R_G2.ap[:, 0:NT], R_G.ap[:, 0:NT], reads=[R_G.res], writes=[R_G2.res])
                    g23 = R_G2.ap[:, 0:NT].rearrange("p (j t) -> p j t", t=8)
                    tt("dve", g23[:, :, 0], g23[:, :, 0], SM0.ap, ALU.max, reads=[R_G2.res, SM0.res], writes=[R_G2.res])
                    scan(R_M.ap[:, 0:NT], CF[0:4, CF_RNEG8:CF_RNEG8 + 128], R_G2.ap[:, 0:NT], 0.0, ALU.add, ALU.max,
                         reads=[CFr, R_G2.res], writes=[R_M.res])
                    cp("dve", R_Mc.ap[:, 0:NBb], SM0.ap, reads=[SM0.res], writes=[R_Mc.res])
                else:
                    scan(R_M.ap[:, 0:NT], CF[0:4, CF_ZERO:CF_ZERO + 1].to_broadcast([4, NT]), R_G.ap[:, 0:NT], MCAR.ap, ALU.add, ALU.max,
                         reads=[CFr, R_G.res, MCAR.res], writes=[R_M.res])
                    m3 = R_M.ap[:, 0:NT].rearrange("p (j t) -> p j t", t=L)
                    cp("dve", R_Mc.ap[:, 0:1], MCAR.ap, reads=[MCAR.res], writes=[R_Mc.res])
                    cp("dve", R_Mc.ap[:, 1:NBb], m3[:, 0:NBb - 1, L - 1], reads=[R_M.res], writes=[R_Mc.res])
                m3 = R_M.ap[:, 0:NT].rearrange("p (j t) -> p j t", t=L)
                f3 = R_F.ap[:, 0:NT].rearrange("p (j t) -> p j t", t=L)
                g3 = R_G.ap[:, 0:NT].rearrange("p (j t) -> p j t", t=L)
                Mc_b = R_Mc.ap[:, 0:NBb].unsqueeze(2).to_broadcast([4, NBb, L])
                Me_b = m3[:, :, L - 1:L].to_broadcast([4, NBb, L])
                R_fe = RSM[1]
                tt("dve", R_fe.ap[:, 0:NBb], R_Mc.ap[:, 0:NBb], m3[:, :, L - 1], ALU.subtract, reads=[R_Mc.res, R_M.res], writes=[R_fe.res])
                act(R_fe.ap[:, 0:NBb], R_fe.ap[:, 0:NBb], AF.Exp, reads=[R_fe.res], writes=[R_fe.res])
                R_es = RSB[0]
                es3 = R_es.ap[:, 0:NT].rearrange("p (j t) -> p j t", t=L)
                tt("dve", es3, g3, Mc_b, ALU.subtract, reads=[R_G.res, R_Mc.res], writes=[R_es.res])
                ts("dve", R_es.ap[:, 0:NT], R_es.ap[:, 0:NT], 60.0, LNS, ALU.min, ALU.add, reads=[R_es.res], writes=[R_es.res])
                act(R_es.ap[:, 0:NT], R_es.ap[:, 0:NT], AF.Exp, reads=[R_es.res], writes=[R_es.res])
                R_we = RSB[1]
                we3 = R_we.ap[:, 0:NT].rearrange("p (j t) -> p j t", t=L)
                tt("dve", we3, g3, Me_b, ALU.subtract, reads=[R_G.res, R_M.res], writes=[R_we.res])
                act(R_we.ap[:, 0:NT], R_we.ap[:, 0:NT], AF.Exp, reads=[R_we.res, CFr], writes=[R_we.res], bias=CF[0:4, CF_LNS:CF_LNS + 1])
                R_fl = RSB[5]
                fl3 = R_fl.ap[:, 0:NT].rearrange("p (j t) -> p j t", t=L)
                tt("dve", fl3, f3, Mc_b, ALU.add, reads=[R_F.res, R_Mc.res], writes=[R_fl.res])
                act(R_fl.ap[:, 0:NT], R_fl.ap[:, 0:NT], AF.Exp, reads=[R_fl.res], writes=[R_fl.res], scale=-1.0)
                if is_s:
                    R_mo = RSM[2]
                    tt("dve", R_mo.ap[:, 0:16], f3[:, :, L - 1], m3[:, :, L - 1], ALU.add, reads=[R_F.res, R_M.res], writes=[R_mo.res])
                    odma(om_s[:, :], R_mo.ap[:, 0:16], reads=[R_mo.res])
                else:
                    tt("dve", MOUT.ap, R_F.ap[:, NT - 1:NT], R_M.ap[:, NT - 1:NT], ALU.add, reads=[R_F.res, R_M.res], writes=[MOUT.res])
                    cp("dve", MCAR.ap, MOUT.ap, reads=[MOUT.res], writes=[MCAR.res])
                R_fd = RSM[3]
                fd3 = R_fd.ap[:, 0:4 * NBb].rearrange("p (h j) -> p h j", j=NBb)
                tt("dve", fd3, R_fe.ap[:, 0:NBb].unsqueeze(1).to_broadcast([4, 4, NBb]),
                   CF[0:4, CF_IDENT:CF_IDENT + 4].unsqueeze(2).to_broadcast([4, 4, NBb]), ALU.mult,
                   reads=[R_fe.res, CFr], writes=[R_fd.res])
                yield

                def hf_evac(h):
                    bk = PBIG.get()
                    p_f = fm_proj(512 + h * 128, 128, bk, WIN_res[1])
                    E = EV.get()
                    act(E.ap[:, 0:NT], p_f, AF.Exp, reads=[PSr[bk]], writes=[E.res], scale=-1.0)
                    return E

                def hf_chain(h, E):
                    A = TF.get()
                    act(A.ap[:, 0:NT], E.ap[:, 0:NT], AF.Ln, reads=[E.res, SMr, CFr], writes=[A.res],
                        scale=LB[:, h:h + 1], bias=CF[:, CF_ONE:CF_ONE + 1])
                    C = TF.get()
                    act(C.ap[:, 0:NT], E.ap[:, 0:NT], AF.Ln, reads=[E.res, CFr], writes=[C.res], bias=CF[:, CF_ONE:CF_ONE + 1])
                    tt("dve", A.ap[:, 0:NT], A.ap[:, 0:NT], C.ap[:, 0:NT], ALU.subtract, reads=[A.res, C.res], writes=[A.res])
                    T4 = TF.get()
                    scan(T4.ap[:, 0:NT], RM, A.ap[:, 0:NT], 0.0, ALU.mult, ALU.add, reads=[CFr, A.res], writes=[T4.res])
                    b3 = T4.ap[:, 0:NT].rearrange("p (j t) -> p j t", t=L)
                    act(A.ap[:, 0:NT], A.ap[:, 0:NT], AF.Exp, reads=[A.res], writes=[A.res])
                    ts("pool", A.ap[:, 0:NT], A.ap[:, 0:NT], -1.0, 1.0, ALU.mult, ALU.add, reads=[A.res], writes=[A.res])
                    act(C.ap[:, 0:NT], T4.ap[:, 0:NT], AF.Exp, reads=[T4.res, C.res], writes=[C.res], scale=-1.0)
                    tt("pool", KT.ap[:, h, 0:NT], A.ap[:, 0:NT], C.ap[:, 0:NT], ALU.mult, reads=[A.res, C.res], writes=[KTr[h]])
                    T7 = TF.get()
                    t73 = T7.ap[:, 0:NT].rearrange("p (j t) -> p j t", t=L)
                    tt("dve", t73, b3[:, :, L - 1:L].to_broadcast([P, NBb, L]), b3, ALU.subtract, reads=[T4.res], writes=[T7.res])
                    act(T7.ap[:, 0:NT], T7.ap[:, 0:NT], AF.Exp, reads=[T7.res], writes=[T7.res])
                    tt("pool", KH.ap[:, h, 0:NT], A.ap[:, 0:NT], T7.ap[:, 0:NT], ALU.mult, reads=[A.res, T7.res], writes=[KHr[h]])
                    act(ELAST.ap[:, h, 0:NBb], b3[:, :, L - 1], AF.Exp, reads=[T4.res], writes=[ELr[h]])
                    eb = EBR.get()
                    EBs[h] = eb
                    act(eb.ap[:, 0:NT], T4.ap[:, 0:NT], AF.Exp, reads=[T4.res], writes=[eb.res])
                for h in range(4):
                    bk = PBIG.get()
                    p = fm_proj(2048 + h * 128, 128, bk, WIN_res[4])
                    cp("act", MQ.ap[:, h, 0:NT], p, reads=[PSr[bk]], writes=[MQr[h]])
                    yield
                    bk = PBIG.get()
                    p = fm_proj(2560 + h * 128, 128, bk, WIN_res[5])
                    cp("act", MK.ap[:, h, 0:NT], p, reads=[PSr[bk]], writes=[MKr[h]])
                    yield
                def hq_unit(h):
                    bq = PBIG.get()
                    p_q = fm_proj(h * 128, 128, bq, WIN_res[0])
                    qs = QSR.get()
                    eb = EBs.pop(h)
                    cp("act", qs.ap[:, 0:NT], p_q, reads=[PSr[bq]], writes=[qs.res])
                    tt("pool", QT.ap[:, h, 0:NT], qs.ap[:, 0:NT], eb.ap[:, 0:NT], ALU.mult, reads=[qs.res, eb.res], writes=[QTr[h]])
                Es = {0: hf_evac(0)}
                yield
                for h in range(4):
                    if h + 1 < 4:
                        Es[h + 1] = hf_evac(h + 1)
                    if h >= 1:
                        hq_unit(h - 1)
                        yield
                    hf_chain(h, Es.pop(h))
                    yield
                hq_unit(3)
                yield
                ptfs = [PS[PTR][:, 256:268], PS[PTR][:, 468:480]]
                for bi in range(nt):
                    for qi, R_q in enumerate((R_es, R_we, R_fl)):
                        tr(ptfs[bi][:, qi * 4:(qi + 1) * 4], R_q.ap[:, bi * P:(bi + 1) * P], CF[0:4, CF_IDENT:CF_IDENT + 4],
                           reads=[R_q.res, CFr], writes=[PSr[PTR]])
                pfe = PS[PTR][:, 276:276 + 4 * NBb]
                mm(pfe, CF[0:4, CF_ONEROW:CF_ONEROW + 128], R_fd.ap[:, 0:4 * NBb], True, True,
                   reads=[CFr, R_fd.res], writes=[PSr[PTR]])
                for bi in range(nt):
                    cp("dve", TSC[bi].ap, ptfs[bi], reads=[PSr[PTR]], writes=[TSC[bi].res])
                cp("dve", FEB.ap[:, :, 0:NBb], pfe.rearrange("p (h j) -> p h j", j=NBb), reads=[PSr[PTR]], writes=[FEB.res])
                yield

            def gen_B2(blk, bs, bi, out):
                hT = bs["HT"]
                tok = slice(bi * P, (bi + 1) * P)
                hTr = bs["HTr"][bi]
                tsc = bs["TSC"][bi]

                def tm_proj(g):
                    bank = PBIG.get()
                    for kc in range(NKC):
                        mm(PS[bank][:, :], hT.ap[:, kc, tok], WIN[:, kc, g * 512:(g + 1) * 512], kc == 0, kc == NKC - 1,
                           reads=[WIN_res[g]] + hTr, writes=[PSr[bank]])
                    return bank
                bk = tm_proj(2)
                vhg = VHG.get()
                cp("dve", vhg.ap, PS[bk][:, :], reads=[PSr[bk]], writes=[vhg.res])
                out["vhg"] = vhg
                yield
                bk = tm_proj(3)
                tsg = TSG.get()
                act(tsg.ap, PS[bk][:, :], AF.Sigmoid, reads=[PSr[bk]], writes=[tsg.res])
                sgt = SGT.get()
                tt("dve", sgt.ap, PS[bk][:, :], tsg.ap, ALU.mult, reads=[PSr[bk], tsg.res], writes=[sgt.res])
                out["sgt"] = sgt
                yield
                bk = tm_proj(6)
                vml = VML.get()
                cp("dve", vml.ap, PS[bk][:, :], reads=[PSr[bk]], writes=[vml.res])
                out["vml"] = vml
                yield
                bk = tm_proj(7)
                so = SO.get()
                act(so.ap, PS[bk][:, :], AF.Sigmoid, reads=[PSr[bk]], writes=[so.res])
                out["so"] = so
                yield
                bk = tm_proj(5)
                kpp = KPP.get()
                for h in range(4):
                    act(kpp.ap[:, h, :], PS[bk][:, h * 128:(h + 1) * 128], AF.Copy, reads=[PSr[bk], tsc.res], writes=[kpp.res],
                        scale=tsc.ap[:, 4 + h:5 + h])
                out["kpp"] = kpp
                yield

            def gen_R(blk, bs, bi, tb):
                ti = blk[bi]
                xb = X2b[slots[ti]]
                tok = slice(bi * P, (bi + 1) * P)
                tsc = bs["TSC"][bi]
                QT, KT, KH, MQ, MK = bs["QT"], bs["KT"], bs["KH"], bs["MQ"], bs["MK"]
                QTr, KTr, KHr, MQr, MKr = bs["QTr"], bs["KTr"], bs["KHr"], bs["MQr"], bs["MKr"]
                ELAST, ELr, FEB = bs["ELAST"], bs["ELr"], bs["FEB"]
                MASK = CF[:, CF_MASK64:CF_MASK64 + 128]
                vhg, sgt, kpp, vml, so = tb["vhg"], tb["sgt"], tb["kpp"], tb["vml"], tb["so"]
                tb_ = PBIG.get()
                ptk = psb(tb_, BF16)[:, 0:512].rearrange("p (h k) -> p h k", k=128)
                for h in range(4):
                    tr(ptk[:, h, :], KH.ap[:, h, tok], IDB, reads=[KHr[h], CBr], writes=[PSr[tb_]])
                cp("act", KHTOK.ap, ptk, reads=[PSr[tb_]], writes=[KHTOK.res])
                pa = PA.get()
                pa3 = PS[pa][:, :].rearrange("p (h t) -> p h t", t=128)
                for h in range(4):
                    mm(pa3[:, h, :], KT.ap[:, h, tok], QT.ap[:, h, tok], True, True, reads=[KTr[h], QTr[h]], writes=[PSr[pa]])
                atbh = ATB.get()
                tt("dve", atbh.ap, pa3, MASK.unsqueeze(1).to_broadcast([P, 4, 128]), ALU.mult, reads=[PSr[pa], CFr], writes=[atbh.res])
                pa = PA.get()
                pa3 = PS[pa][:, :].rearrange("p (h t) -> p h t", t=128)
                for h in range(4):
                    mm(pa3[:, h, :], MK.ap[:, h, tok], MQ.ap[:, h, tok], True, True, reads=[MKr[h], MQr[h]], writes=[PSr[pa]])
                atbm = ATB.get()
                for h in range(4):
                    act(atbm.ap[:, h, :], pa3[:, h, :], AF.Copy, reads=[PSr[pa], tsc.res], writes=[atbm.res], scale=tsc.ap[:, h:h + 1])
                tt("pool", atbm.ap, atbm.ap, MASKB.unsqueeze(1).to_broadcast([P, 4, 128]), ALU.mult, reads=[atbm.res, CBr], writes=[atbm.res])
                yield

                def hg_update(j, dst_bf):
                    bank = PO_H if j == 0 else PA.get()
                    pu = PS[bank][:, :].rearrange("p (h v) -> p h v", v=128)
                    rows = slice(j * 64, (j + 1) * 64)
                    for h in range(4):
                        mm(pu[:, h, :], KHTOK.ap[rows, h, :], vhg.ap[rows, h * 128:(h + 1) * 128], h == 0, h == 3,
                           reads=[KHTOK.res, vhg.res], writes=[PSr[bank]])
                    for h in range(4):
                        stt(SF.ap[:, h, :], SF.ap[:, h, :], ELAST.ap[:, h, bi * 2 + j:bi * 2 + j + 1], pu[:, h, :], ALU.mult, ALU.add,
                            reads=[SF.res, ELr[h], PSr[bank]], writes=[SF.res])
                    cp("dve", dst_bf.ap, SF.ap, reads=[SF.res], writes=[dst_bf.res])

                def ml_update(j, dst_c, dst_n):
                    bank = PO_M if j == 0 else PA.get()
                    pu = PS[bank][:, :].rearrange("p (h v) -> p h v", v=128)
                    rows = slice(j * 64, (j + 1) * 64)
                    for h in range(4):
                        mm(pu[:, h, :], kpp.ap[rows, h, :], vml.ap[rows, h * 128:(h + 1) * 128], h == 0, h == 3,
                           reads=[kpp.res, vml.res], writes=[PSr[bank]])
                    pnn = PS[PTR][:, 272:276]
                    for h in range(4):
                        mm(pnn[:, h:h + 1], kpp.ap[rows, h, :], ONEC.ap[rows, 0:1], True, True,
                           reads=[kpp.res, ONEC.res], writes=[PSr[PTR]])
                    jj = bi * 2 + j
                    for h in range(4):
                        stt(CFs.ap[:, h, :], CFs.ap[:, h, :], FEB.ap[:, h, jj:jj + 1], pu[:, h, :], ALU.mult, ALU.add,
                            reads=[CFs.res, FEB.res, PSr[bank]], writes=[CFs.res])
                    cp("dve", dst_c.ap, CFs.ap, reads=[CFs.res], writes=[dst_c.res])
                    tt("dve", NF.ap, NF.ap, FEB.ap[:, :, jj], ALU.mult, reads=[NF.res, FEB.res], writes=[NF.res])
                    tt("dve", NF.ap, NF.ap, pnn, ALU.add, reads=[NF.res, PSr[PTR]], writes=[NF.res])
                    cp("dve", dst_n.ap, NF.ap, reads=[NF.res], writes=[dst_n.res])
                hg_update(0, SBF[1])
                ml_update(0, CBFs[1], NBF[1])
                yield
                po3 = PS[PO_H][:, :].rearrange("p (h v) -> p h v", v=128)
                for h in range(4):
                    mm(po3[:, h, :], atbh.ap[:, h, :], vhg.ap[:, h * 128:(h + 1) * 128], h == 0, False,
                       reads=[atbh.res, vhg.res], writes=[PSr[PO_H]])
                    mm(po3[0:64, h, :], QT.ap[:, h, bi * P:bi * P + 64], SBF[0].ap[:, h, :], False, True,
                       reads=[QTr[h], SBF[0].res], writes=[PSr[PO_H]])
                    mm(po3[64:128, h, :], QT.ap[:, h, bi * P + 64:bi * P + 128], SBF[1].ap[:, h, :], False, True,
                       reads=[QTr[h], SBF[1].res], writes=[PSr[PO_H]])
                sq = SSQ.get()
                for h in range(4):
                    act(JUNK.ap[:, 0:128], po3[:, h, :], AF.Square, reads=[PSr[PO_H]], writes=[JUNK.res, sq.res],
                        accum_out=sq.ap[:, h:h + 1])
                ts("dve", sq.ap[:, 4:8], sq.ap[:, 0:4], 1.0 / 128, EPS, ALU.mult, ALU.add, reads=[sq.res], writes=[sq.res])
                tt("pool", sq.ap[:, 4:8], sq.ap[:, 4:8], CF[:, CF_NHALF:CF_NHALF + 4], ALU.pow, reads=[sq.res, CFr], writes=[sq.res])
                for h in range(4):
                    stt(OCAT.ap[:, h * 128:(h + 1) * 128], po3[:, h, :], sq.ap[:, 4 + h:5 + h], sgt.ap[:, h * 128:(h + 1) * 128],
                        ALU.mult, ALU.mult, reads=[PSr[PO_H], sq.res, sgt.res], writes=[OCAT.res])
                yield
                pn3 = PS[PO_M][:, :].rearrange("p (h v) -> p h v", v=128)
                pd = PS[PTR][:, 268:272]
                for h in range(4):
                    mm(pn3[:, h, :], atbm.ap[:, h, :], vml.ap[:, h * 128:(h + 1) * 128], h == 0, False,
                       reads=[atbm.res, vml.res], writes=[PSr[PO_M]])
                    mm(pn3[0:64, h, :], MQ.ap[:, h, bi * P:bi * P + 64], CBFs[0].ap[:, h, :], False, True,
                       reads=[MQr[h], CBFs[0].res], writes=[PSr[PO_M]])
                    mm(pn3[64:128, h, :], MQ.ap[:, h, bi * P + 64:bi * P + 128], CBFs[1].ap[:, h, :], False, True,
                       reads=[MQr[h], CBFs[1].res], writes=[PSr[PO_M]])
                for h in range(4):
                    mm(pd[:, h:h + 1], atbm.ap[:, h, :], ONEC.ap[:, 0:1], True, False, reads=[atbm.res, ONEC.res], writes=[PSr[PTR]])
                    mm(pd[0:64, h:h + 1], MQ.ap[:, h, bi * P:bi * P + 64], NBF[0].ap[:, h:h + 1], False, True,
                       reads=[MQr[h], NBF[0].res], writes=[PSr[PTR]])
                    mm(pd[64:128, h:h + 1], MQ.ap[:, h, bi * P + 64:bi * P + 128], NBF[1].ap[:, h:h + 1], False, True,
                       reads=[MQr[h], NBF[1].res], writes=[PSr[PTR]])
                ml_norm(pn3, pd, PSr[PTR], tsc, so)
                yield
                hg_update(1, SBF[0])
                ml_update(1, CBFs[0], NBF[0])
                yield
                for u in out_proj(xb):
                    yield

            def ml_norm(pn3, pd, pd_res, tsc, so):
                sq = SSQ.get()
                for h in range(4):
                    act(JUNK.ap[:, 0:128], pn3[:, h, :], AF.Square, reads=[PSr[PO_M]], writes=[JUNK.res, sq.res],
                        accum_out=sq.ap[:, h:h + 1])
                sq2 = SSQ.get()
                ts("dve", sq2.ap[:, 4:8], pd, -1.0, None, ALU.mult, None, reads=[pd_res], writes=[sq2.res])
                tt("dve", sq2.ap[:, 0:4], pd, sq2.ap[:, 4:8], ALU.max, reads=[pd_res, sq2.res], writes=[sq2.res])
                tt("dve", sq2.ap[:, 0:4], sq2.ap[:, 0:4], tsc.ap[:, 8:12], ALU.max, reads=[sq2.res, tsc.res], writes=[sq2.res])
                tt("dve", sq2.ap[:, 0:4], sq2.ap[:, 0:4], sq2.ap[:, 0:4], ALU.mult, reads=[sq2.res], writes=[sq2.res])
                ts("dve", sq.ap[:, 4:8], sq.ap[:, 0:4], 1.0 / 128, None, ALU.mult, None, reads=[sq.res], writes=[sq.res])
                stt(sq.ap[:, 4:8], sq2.ap[:, 0:4], EPS, sq.ap[:, 4:8], ALU.mult, ALU.add, reads=[sq.res, sq2.res], writes=[sq.res])
                tt("pool", sq.ap[:, 4:8], sq.ap[:, 4:8], CF[:, CF_NHALF:CF_NHALF + 4], ALU.pow, reads=[sq.res, CFr], writes=[sq.res])
                for h in range(4):
                    stt(OCAT.ap[:, 512 + h * 128:512 + (h + 1) * 128], pn3[:, h, :], sq.ap[:, 4 + h:5 + h], so.ap[:, h * 128:(h + 1) * 128],
                        ALU.mult, ALU.mult, reads=[PSr[PO_M], sq.res, so.res], writes=[OCAT.res])

            def out_proj(xb):
                for pk in range(2):
                    tb_ = PBIG.get()
                    pt = psb(tb_, BF16)[:, 0:512].rearrange("p (k t) -> p k t", t=128)
                    for k4 in range(4):
                        kc = pk * 4 + k4
                        tr(pt[:, k4, :], OCAT.ap[:, kc * P:(kc + 1) * P], IDB, reads=[OCAT.res, CBr], writes=[PSr[tb_]])
                    cp("act", OT.ap[:, pk * 4:(pk + 1) * 4, :], pt, reads=[PSr[tb_]], writes=[OT.res])
                yield
                for half in range(2):
                    bank = PBIG.get()
                    for kc in range(NKC):
                        mm(PS[bank][:, :], OT.ap[:, kc, :], WOUT[:, kc, half * 512:(half + 1) * 512], kc == 0, kc == NKC - 1,
                           reads=[OT.res, WOUT_res[kc]], writes=[PSr[bank]])
                    tt("dve", xb.ap[:, half * 512:(half + 1) * 512], xb.ap[:, half * 512:(half + 1) * 512], PS[bank][:, :], ALU.add,
                       reads=[xb.res, PSr[bank]], writes=[xb.res])
                yield

            def run_sample(blk, bs, tb, soft_deps):
                ti = blk[0]
                xb = X2b[slots[ti]]
                tok = slice(0, P)
                tsc = bs["TSC"][0]
                QT, KT, KH, MQ, MK = bs["QT"], bs["KT"], bs["KH"], bs["MQ"], bs["MK"]
                QTr, KTr, KHr, MQr, MKr = bs["QTr"], bs["KTr"], bs["KHr"], bs["MQr"], bs["MKr"]
                ELAST, ELr, FEB = bs["ELAST"], bs["ELr"], bs["FEB"]
                MASK = CF[:, CF_MASK8:CF_MASK8 + 128]
                vhg, sgt, kpp, vml, so = tb["vhg"], tb["sgt"], tb["kpp"], tb["vml"], tb["so"]
                tb_ = PBIG.get()
                ptk = psb(tb_, BF16)[:, 0:512].rearrange("p (h k) -> p h k", k=128)
                for h in range(4):
                    tr(ptk[:, h, :], KH.ap[:, h, tok], IDB, reads=[KHr[h], CBr], writes=[PSr[tb_]])
                cp("act", KHTOK.ap, ptk, reads=[PSr[tb_]], writes=[KHTOK.res])
                pa = PA.get()
                pa3 = PS[pa][:, :].rearrange("p (h t) -> p h t", t=128)
                for h in range(4):
                    mm(pa3[:, h, :], KT.ap[:, h, tok], QT.ap[:, h, tok], True, True, reads=[KTr[h], QTr[h]], writes=[PSr[pa]])
                atb = ATB.get()
                tt("dve", atb.ap, pa3, MASK.unsqueeze(1).to_broadcast([P, 4, 128]), ALU.mult, reads=[PSr[pa], CFr], writes=[atb.res])
                po3 = PS[PO_H][:, :].rearrange("p (h v) -> p h v", v=128)
                for h in range(4):
                    mm(po3[:, h, :], atb.ap[:, h, :], vhg.ap[:, h * 128:(h + 1) * 128], h == 0, False,
                       reads=[atb.res, vhg.res], writes=[PSr[PO_H]])

                def state_loop(src_state, dst_state, QX, QXr, kmat, kres, vmat, po, po_res, dec, extra=None):
                    if soft_deps:
                        S.fence_deps = list(S.fence_deps) + soft_deps.pop()
                    QM.i = 0
                    for qb in QM.bufs:
                        memset("pool", qb.ap, 0.0, writes=[qb.res])
                    PRE = 2
                    prepd = {}

                    def prep(j):
                        ssf = SSF.get()
                        dma("sp", ssf.ap, src_state[j].rearrange("h k v -> k h v"), writes=[ssf.res])
                        ssb = SSB.get()
                        cp("act", ssb.ap, ssf.ap, reads=[ssf.res], writes=[ssb.res])
                        qm = QM.get()
                        if j >= len(QM.bufs):
                            jo = j - len(QM.bufs)
                            memset("pool", qm.ap[:, :, jo * 8:(jo + 1) * 8], 0.0, writes=[qm.res])
                        cp("dve", qm.ap[:, :, j * 8:(j + 1) * 8], QX.ap[:, :, j * 8:(j + 1) * 8], reads=QXr, writes=[qm.res])
                        km = KM.get()
                        ts("pool", km.ap, kmat.ap, CF[:, CF_ROWMASK + j:CF_ROWMASK + j + 1], None, ALU.mult, None,
                           reads=[kres, CFr], writes=[km.res])
                        prepd[j] = (ssf, ssb, qm, km)
                    for j in range(PRE):
                        prep(j)
                    for j in range(16):
                        ssf, ssb, qm, km = prepd.pop(j)
                        for h in range(4):
                            mm(po[:, h, :], qm.ap[:, h, :], ssb.ap[:, h, :], False, j == 15,
                               reads=[qm.res, ssb.res], writes=[po_res])
                        bank = PBIG.get()
                        pu = PS[bank][:, :].rearrange("p (h v) -> p h v", v=128)
                        for h in range(4):
                            mm(pu[:, h, :], km.ap[:, h, :], vmat.ap[:, h * 128:(h + 1) * 128], h == 0, h == 3,
                               reads=[km.res, vmat.res], writes=[PSr[bank]])
                        if extra is not None:
                            extra(j, km, 0)
                        if j + PRE < 16:
                            prep(j + PRE)
                        ssn = SSN.get()
                        for h in range(4):
                            sc_ap, sc_res = dec(j, h)
                            stt(ssn.ap[:, h, :], ssf.ap[:, h, :], sc_ap, pu[:, h, :], ALU.mult, ALU.add,
                                reads=[ssf.res, sc_res, PSr[bank]], writes=[ssn.res])
                        odma(dst_state[j].rearrange("h k v -> k h v"), ssn.ap, reads=[ssn.res])
                        if extra is not None:
                            extra(j, km, 1)

                state_loop(sS, oS_s, QT, QTr, KHTOK, KHTOK.res, vhg, po3, PSr[PO_H],
                           lambda j, h: (ELAST.ap[:, h, j:j + 1], ELr[h]))
                sq = SSQ.get()
                for h in range(4):
                    act(JUNK.ap[:, 0:128], po3[:, h, :], AF.Square, reads=[PSr[PO_H]], writes=[JUNK.res, sq.res],
                        accum_out=sq.ap[:, h:h + 1])
                ts("dve", sq.ap[:, 4:8], sq.ap[:, 0:4], 1.0 / 128, EPS, ALU.mult, ALU.add, reads=[sq.res], writes=[sq.res])
                tt("pool", sq.ap[:, 4:8], sq.ap[:, 4:8], CF[:, CF_NHALF:CF_NHALF + 4], ALU.pow, reads=[sq.res, CFr], writes=[sq.res])
                for h in range(4):
                    stt(OCAT.ap[:, h * 128:(h + 1) * 128], po3[:, h, :], sq.ap[:, 4 + h:5 + h], sgt.ap[:, h * 128:(h + 1) * 128],
                        ALU.mult, ALU.mult, reads=[PSr[PO_H], sq.res, sgt.res], writes=[OCAT.res])
                pa = PA.get()
                pa3 = PS[pa][:, :].rearrange("p (h t) -> p h t", t=128)
                for h in range(4):
                    mm(pa3[:, h, :], MK.ap[:, h, tok], MQ.ap[:, h, tok], True, True, reads=[MKr[h], MQr[h]], writes=[PSr[pa]])
                atb = ATB.get()
                for h in range(4):
                    stt(atb.ap[:, h, :], pa3[:, h, :], tsc.ap[:, h:h + 1], MASK, ALU.mult, ALU.mult,
                        reads=[PSr[pa], tsc.res, CFr], writes=[atb.res])
                pn3 = PS[PO_M][:, :].rearrange("p (h v) -> p h v", v=128)
                dma("sp", SN_ROW.ap[0:64, :], sn[:, :], writes=[SN_ROW.res])
                psn = PS[PTR][:, 340:404]
                tr(psn, SN_ROW.ap[0:64, :], CF[0:64, CF_IDENT:CF_IDENT + 64], reads=[SN_ROW.res, CFr], writes=[PSr[PTR]])
                cp("dve", SN_F.ap, psn, reads=[PSr[PTR]], writes=[SN_F.res])
                cp("dve", SN_B.ap, SN_F.ap, reads=[SN_F.res], writes=[SN_B.res])
                snf3 = SN_F.ap.rearrange("p (j h) -> p j h", h=4)
                snb3 = SN_B.ap.rearrange("p (j h) -> p j h", h=4)
                snn3 = SN_NEW.ap.rearrange("p (j h) -> p j h", h=4)
                for h in range(4):
                    mm(pn3[:, h, :], atb.ap[:, h, :], vml.ap[:, h * 128:(h + 1) * 128], h == 0, False,
                       reads=[atb.res, vml.res], writes=[PSr[PO_M]])

                def n_update(j, km, part):
                    pnn = PS[PTR][:, 272:276]
                    if part == 0:
                        for h in range(4):
                            mm(pnn[:, h:h + 1], km.ap[:, h, :], ONEC.ap[:, 0:1], True, True, reads=[km.res, ONEC.res], writes=[PSr[PTR]])
                    else:
                        tt("dve", snn3[:, j, :], snf3[:, j, :], FEB.ap[:, :, j], ALU.mult, reads=[SN_F.res, FEB.res], writes=[SN_NEW.res])
                        tt("dve", snn3[:, j, :], snn3[:, j, :], pnn, ALU.add, reads=[SN_NEW.res, PSr[PTR]], writes=[SN_NEW.res])
                state_loop(sC, oC_s, MQ, MQr, kpp, kpp.res, vml, pn3, PSr[PO_M],
                           lambda j, h: (FEB.ap[:, h, j:j + 1], FEB.res), extra=n_update)
                pqn = PS[PTR][:, 404:468].rearrange("p (h j) -> p h j", j=16)
                for h in range(4):
                    mm(pqn[:, h, :], MQ.ap[:, h, tok], snb3[:, :, h], True, True, reads=[MQr[h], SN_B.res], writes=[PSr[PTR]])
                qn = SSQ.get()
                qn3 = SN_QN.ap.rearrange("p (h j) -> p h j", j=16)
                tt("dve", qn3, pqn, CF[:, CF_ROWMASK:CF_ROWMASK + 16].unsqueeze(1).to_broadcast([P, 4, 16]), ALU.mult,
                   reads=[PSr[PTR], CFr], writes=[SN_QN.res])
                treduce(qn.ap[:, 0:4], qn3, reads=[SN_QN.res], writes=[qn.res])
                pd = PS[PTR][:, 268:272]
                for h in range(4):
                    mm(pd[:, h:h + 1], atb.ap[:, h, :], ONEC.ap[:, 0:1], True, True, reads=[atb.res, ONEC.res], writes=[PSr[PTR]])
                pd_sb = SSQ.get()
                tt("dve", pd_sb.ap[:, 0:4], pd, qn.ap[:, 0:4], ALU.add, reads=[PSr[PTR], qn.res], writes=[pd_sb.res])
                pb = PBIG.get()
                tr(PS[pb][0:64, 0:128], SN_NEW.ap, IDF, reads=[SN_NEW.res, CFr], writes=[PSr[pb]])
                cp("dve", SN_ROW.ap[0:64, :], PS[pb][0:64, 0:128], reads=[PSr[pb]], writes=[SN_ROW.res])
                odma(on_s[:, :], SN_ROW.ap[0:64, :], reads=[SN_ROW.res])
                ml_norm(pn3, pd_sb.ap[:, 0:4], pd_sb.res, tsc, so)
                for _ in out_proj(xb):
                    pass

            pblocks = [b for b in blocks if tiles[b[0]] != SAMPLE]
            sblocks = [b for b in blocks if tiles[b[0]] == SAMPLE]
            assert len(sblocks) <= 1 and pblocks
            import itertools
            flat = [(k, bi) for k, b in enumerate(pblocks) for bi in range(len(b))]
            tbs = [dict() for _ in flat]
            tb_s = {}
            ks = len(pblocks)

            allblocks = [(pblocks[k], sets[k % 2], "p%d" % k) for k in range(len(pblocks))]
            if sblocks:
                allblocks.append((sblocks[0], sets[ks % 2], "s"))

            def fillers_for(n):
                gens = []
                if n == len(flat):
                    sb_, bs_, key = allblocks[ks]
                    return itertools.chain(lab(gen_At(sb_, bs_, key), "Ats"), lab(gen_B1(sb_, bs_), "B1s"), lab(gen_B2(sb_, bs_, 0, tb_s), "B2s"))
                k, bi = flat[n]
                blk_, bs_, key = allblocks[k]
                if bi == 0:
                    gens.append(lab(gen_At(blk_, bs_, key), "At%d" % k))
                    gens.append(lab(gen_B1(blk_, bs_), "B1_%d" % k))
                gens.append(lab(gen_B2(blk_, bs_, bi, tbs[n]), "B2_%d" % n))
                if bi == len(blk_) - 1 and k + 1 < len(allblocks):
                    nb_, nbs_, nkey = allblocks[k + 1]
                    gens.append(lab(gen_Ac(nb_, nbs_, nkey), "Ac%d" % (k + 1)))
                return itertools.chain(*gens)

            def count_units(n):
                if n == len(flat):
                    return 5 + 2 + 19
                k, bi = flat[n]
                c = 5
                if bi == 0:
                    c += 2 * len(pblocks[k]) + 19
                if bi == len(pblocks[k]) - 1 and k + 1 < len(allblocks):
                    c += len(allblocks[k + 1][0])
                return c

            ntot = len(flat) + (1 if sblocks else 0)
            for _ in lab(gen_Ac(allblocks[0][0], allblocks[0][1], allblocks[0][2]), "Ac0"):
                pass
            f0 = fillers_for(0)
            for _ in f0:
                pass
            if first:
                load_wout()
            for n, (k, bi) in enumerate(flat):
                main = lab(gen_R(pblocks[k], sets[k % 2], bi, tbs[n]), "R%d" % n)
                if n + 1 < ntot:
                    fill = fillers_for(n + 1)
                    nun = count_units(n + 1)
                else:
                    fill = iter(())
                    nun = 0
                nsteps = 13
                per = -(-nun // nsteps) if nun else 0
                for step in main:
                    for _ in range(per):
                        try:
                            next(fill)
                        except StopIteration:
                            break
                for _ in fill:
                    pass
            if sblocks:
                deps = []
                for e in ENGS:
                    for op in reversed(S.streams[e]):
                        if not op.is_dma:
                            deps.append(op)
                            break
                deps.extend(S.dmas_since_fence)
                S.tag = "SAMPLE"
                run_sample(sblocks[0], sets[ks % 2], tb_s, [deps])

        def phase2(tiles, slots, last):
            cv = Carver()
            S.tag = "P2"
            ntl = len(tiles)
            NTOK = ntl * P
            H2T = cv.take([NKC, NTOK], BF16, "h2T")
            H2r = [Res("h2t%d" % i) for i in range(ntl)]
            HBF = [cv.take([D], BF16, "hbf%d" % i) for i in range(2)]
            JUNK = HBF[0]
            XSS = Ring([cv.take([2], F32, "xss%d" % i) for i in range(4)])
            WG = Ring([cv.take([NKC, GFF * P], BF16, "wg%d" % i) for i in range(2)])
            WV = Ring([cv.take([NKC, GFF * P], BF16, "wv%d" % i) for i in range(2)])
            WD = Ring([cv.take([GFF, D], BF16, "wd%d" % i) for i in range(2)])
            GT = Ring([cv.take([GFF, NTOK], BF16, "gT%d" % i) for i in range(2)])
            UB = Ring([cv.take([2 + 512], F32, "ub%d" % i) for i in range(2)])
            ACC = Ring([cv.take([512], F32, "acc%d" % i) for i in range(2)])
            CST = cv.take([NFC, 32], F32, "cst")
            CROW = Ring([cv.take([128], F32, "crow%d" % i, parts=32) for i in range(3)])
            CTL = cv.take([NFC, 32], F32, "ctl")
            has_s = SAMPLE in tiles
            blocks = []
            i = 0
            while i < ntl:
                if tiles[i] == SAMPLE:
                    blocks.append([i])
                    i += 1
                else:
                    blocks.append(list(range(i, min(i + 4, ntl))))
                    i += len(blocks[-1])

            last_prompt_b = max(k for k, blk in enumerate(blocks) if tiles[blk[0]] != SAMPLE)
            PU = Ring([0, 1])
            PV = Ring([2, 3])
            PD2 = Ring([4, 5, 6, 7])
            PTR = 0

            dma("sp", GB.ap, g2[0, :].partition_broadcast(P), writes=[GB.res])
            def n2_chain(ti):
                xb = X2b[slots[ti]]
                ss = XSS.get()
                hb = HBF[ti % 2]
                act(hb.ap, xb.ap, AF.Square, reads=[xb.res], writes=[hb.res, ss.res], accum_out=ss.ap[:, 0:1])
                ts("dve", ss.ap[:, 1:2], ss.ap[:, 0:1], 1.0 / D, EPS, ALU.mult, ALU.add, reads=[ss.res], writes=[ss.res])
                tt("pool", ss.ap[:, 1:2], ss.ap[:, 1:2], CF[:, CF_NHALF:CF_NHALF + 1], ALU.pow, reads=[ss.res, CFr], writes=[ss.res])
                stt(hb.ap, xb.ap, ss.ap[:, 1:2], GB.ap, ALU.mult, ALU.mult, reads=[xb.res, ss.res, GB.res], writes=[hb.res])
            n2_chain(0)
            for ti in range(ntl):
                if ti + 1 < ntl:
                    n2_chain(ti + 1)
                hb = HBF[ti % 2]
                bank = PD2.get()
                pt = psb(bank, BF16).rearrange("p (k t) -> p k t", t=128)
                for kc in range(NKC):
                    tr(pt[:, kc, :], hb.ap[:, kc * P:(kc + 1) * P], IDB, reads=[hb.res, CBr], writes=[PSr[bank]])
                cp("act", H2T.ap[:, :, ti * P:(ti + 1) * P], pt, reads=[PSr[bank]], writes=[H2r[ti]])
            def cst_prep(c):
                cr = CROW.get()
                dma("sp", cr.ap, sconv[:, c * P:(c + 1) * P], writes=[cr.res])
                bank = PD2.get()
                tr(PS[bank][:, 0:32], cr.ap, CF[0:32, CF_IDENT:CF_IDENT + 32], reads=[cr.res, CFr], writes=[PSr[bank]])
                cp("dve", CST.ap[:, c, :], PS[bank][:, 0:32], reads=[PSr[bank]], writes=[CST.res])
            if has_s:
                cst_prep(0)

            wg_r = w_gate.rearrange("(kc p) f -> p kc f", p=P)
            wv_r = w_val.rearrange("(kc p) f -> p kc f", p=P)
            wd_r = w_down.rearrange("(c p) d -> p c d", p=P)
            ngroups = NFC // GFF
            for g in range(ngroups):
                wg = WG.get()
                wv = WV.get()
                wd = WD.get()
                f0 = g * GFF * P
                dma("pool", wg.ap, wg_r[:, :, f0:f0 + GFF * P], writes=[wg.res])
                dma("pool", wv.ap, wv_r[:, :, f0:f0 + GFF * P], writes=[wv.res])
                dma("pool", wd.ap, wd_r[:, g * GFF:(g + 1) * GFF, :], writes=[wd.res])
                gt = GT.get()
                gtr = [[Res("gt_%d_%d" % (cc, b)) for b in range(len(blocks))] for cc in range(GFF)]
                for cc in range(GFF):
                    c = g * GFF + cc
                    prev_ub = None
                    if has_s and c + 1 < NFC:
                        cst_prep(c + 1)
                    for b_i, blk in enumerate(blocks):
                        is_s = tiles[blk[0]] == SAMPLE
                        NT = len(blk) * P
                        t0 = blk[0] * P
                        h2r = [H2r[ti] for ti in blk]
                        bu = PU.get()
                        for kc in range(NKC):
                            mm(PS[bu][:, 0:NT], wg.ap[:, kc, cc * P:(cc + 1) * P], H2T.ap[:, kc, t0:t0 + NT], kc == 0, kc == NKC - 1,
                               reads=[wg.res] + h2r, writes=[PSr[bu]])
                        bv = PV.get()
                        for kc in range(NKC):
                            mm(PS[bv][:, 0:NT], wv.ap[:, kc, cc * P:(cc + 1) * P], H2T.ap[:, kc, t0:t0 + NT], kc == 0, kc == NKC - 1,
                               reads=[wv.res] + h2r, writes=[PSr[bv]])
                        ub = UB.get()
                        acc = ACC.get()
                        if is_s:
                            u3 = ub.ap[:, 0:160].rearrange("p (j t) -> p j t", t=10)
                            cp("pool", u3[:, :, 0:2], CST.ap[:, c, :].rearrange("p (j t) -> p j t", t=2), reads=[CST.res], writes=[ub.res])
                            cp("act", u3[:, :, 2:10], PS[bu][:, 0:128].rearrange("p (j t) -> p j t", t=8), reads=[PSr[bu]], writes=[ub.res])
                            a3 = acc.ap[:, 0:128].rearrange("p (j t) -> p j t", t=8)
                            ts("dve", a3, u3[:, :, 2:10], CW[:, c, 2:3], CBIAS[:, c:c + 1], ALU.mult, ALU.add, reads=[ub.res, SMr], writes=[acc.res])
                            stt(a3, u3[:, :, 1:9], CW[:, c, 1:2], a3, ALU.mult, ALU.add, reads=[ub.res, SMr, acc.res], writes=[acc.res])
                            stt(a3, u3[:, :, 0:8], CW[:, c, 0:1], a3, ALU.mult, ALU.add, reads=[ub.res, SMr, acc.res], writes=[acc.res])
                            cp("pool", CTL.ap[:, c, :].rearrange("p (j t) -> p j t", t=2), u3[:, :, 8:10], reads=[ub.res], writes=[CTL.res])
                        else:
                            first_blk = (tiles[blk[0]] == 0)
                            if first_blk or prev_ub is None:
                                if first_blk:
                                    memset("pool", ub.ap[:, 0:2], 0.0, writes=[ub.res])
                                else:
                                    cp("pool", ub.ap[:, 0:2], HALO.ap[:, c, :], reads=[HALO.res], writes=[ub.res])
                            else:
                                cp("pool", ub.ap[:, 0:2], prev_ub.ap[:, 512:514], reads=[prev_ub.res], writes=[ub.res])
                            cp("act", ub.ap[:, 2:2 + NT], PS[bu][:, 0:NT], reads=[PSr[bu]], writes=[ub.res])
                            ts("pool", acc.ap[:, 0:NT], ub.ap[:, 2:2 + NT], CW[:, c, 2:3], CBIAS[:, c:c + 1], ALU.mult, ALU.add,
                               reads=[ub.res, SMr], writes=[acc.res])
                            stt(acc.ap[:, 0:NT], ub.ap[:, 1:1 + NT], CW[:, c, 1:2], acc.ap[:, 0:NT], ALU.mult, ALU.add, reads=[ub.res, SMr, acc.res], writes=[acc.res])
                            stt(acc.ap[:, 0:NT], ub.ap[:, 0:NT], CW[:, c, 0:1], acc.ap[:, 0:NT], ALU.mult, ALU.add, reads=[ub.res, SMr, acc.res], writes=[acc.res])
                            prev_ub = ub
                            last_blk = (tiles[blk[-1]] == NPT - 1)
                            if last_blk:
                                cp("pool", CTL.ap[:, c, 0:2], ub.ap[:, NT:NT + 2], reads=[ub.res], writes=[CTL.res])
                            elif b_i == last_prompt_b:
                                cp("pool", HALO.ap[:, c, :], ub.ap[:, NT:NT + 2], reads=[ub.res], writes=[HALO.res])
                        act(acc.ap[:, 0:NT], acc.ap[:, 0:NT], AF.Gelu_apprx_tanh, reads=[acc.res], writes=[acc.res])
                        tt("dve", gt.ap[:, cc, t0:t0 + NT], acc.ap[:, 0:NT], PS[bv][:, 0:NT], ALU.mult, reads=[acc.res, PSr[bv]], writes=[gtr[cc][b_i]])
                for ti in range(ntl):
                    xb = X2b[slots[ti]]
                    b_i = [k for k, blk in enumerate(blocks) if ti in blk][0]
                    for half in range(2):
                        bank = PD2.get()
                        for cc in range(GFF):
                            mm(PS[bank][:, :], gt.ap[:, cc, ti * P:(ti + 1) * P], wd.ap[:, cc, half * 512:(half + 1) * 512], cc == 0, cc == GFF - 1,
                               reads=[gtr[cc][b_i], wd.res], writes=[PSr[bank]])
                        tt("dve", xb.ap[:, half * 512:(half + 1) * 512], xb.ap[:, half * 512:(half + 1) * 512], PS[bank][:, :], ALU.add,
                           reads=[xb.res, PSr[bank]], writes=[xb.res])
            if has_s:
                for c in range(NFC):
                    bank = PD2.get()
                    tr(PS[bank][0:32, 0:128], CTL.ap[:, c, :], IDF, reads=[CTL.res, CFr], writes=[PSr[bank]])
                    cr = CROW.get()
                    cp("dve", cr.ap, PS[bank][0:32, 0:128], reads=[PSr[bank]], writes=[cr.res])
                    odma(ocv_s[:, c * P:(c + 1) * P], cr.ap, reads=[cr.res])
            if last:
                for c in range(NFC):
                    bank = PD2.get()
                    tr(PS[bank][0:2, 0:128], CTL.ap[:, c, 0:2], IDF, reads=[CTL.res, CFr], writes=[PSr[bank]])
                    cr = CROW.get()
                    cp("dve", cr.ap[0:2, :], PS[bank][0:2, 0:128], reads=[PSr[bank]], writes=[cr.res])
                    odma(ocv_p[:, c * P:(c + 1) * P], cr.ap[0:2, :], reads=[cr.res])
            dma("sp", GB.ap, gf[0, :].partition_broadcast(P), writes=[GB.res])
            fss = {}

            def fn_stats(ti):
                xb = X2b[slots[ti]]
                ss = XSS.get()
                act(JUNK.ap, xb.ap, AF.Square, reads=[xb.res], writes=[JUNK.res, ss.res], accum_out=ss.ap[:, 0:1])
                ts("dve", ss.ap[:, 1:2], ss.ap[:, 0:1], 1.0 / D, EPS, ALU.mult, ALU.add, reads=[ss.res], writes=[ss.res])
                tt("pool", ss.ap[:, 1:2], ss.ap[:, 1:2], CF[:, CF_NHALF:CF_NHALF + 1], ALU.pow, reads=[ss.res, CFr], writes=[ss.res])
                fss[ti] = ss
            fn_stats(0)
            for ti in range(ntl):
                if ti + 1 < ntl:
                    fn_stats(ti + 1)
                tile = tiles[ti]
                xb = X2b[slots[ti]]
                ss = fss.pop(ti)
                stt(xb.ap, xb.ap, ss.ap[:, 1:2], GB.ap, ALU.mult, ALU.mult, reads=[xb.res, ss.res, GB.res], writes=[xb.res])
                dst = y_s[:, :] if tile == SAMPLE else y_p[tile * P:(tile + 1) * P, :]
                odma(dst, xb.ap, reads=[xb.res])


        for si, tl in enumerate(SBS):
            slots = list(range(len(tl)))
            phase1(tl, slots, si == 0)
            if si == len(SBS) - 1:
                odma(oS_p.rearrange("h k v -> k h v"), SF.ap, reads=[SF.res])
                odma(oC_p.rearrange("h k v -> k h v"), CFs.ap, reads=[CFs.res])
                odma(om_p[:, :], MOUT.ap, reads=[MOUT.res])
            S.fence()
            phase2(tl, slots, si == len(SBS) - 1)
            S.fence()
        tr(PS[0][0:4, 0:128], NF.ap, IDF, reads=[NF.res, CFr], writes=[PSr[0]])
        cp("dve", NROW.ap, PS[0][0:4, 0:128], reads=[PSr[0]], writes=[NROW.res])
        odma(on_p[:, :], NROW.ap, reads=[NROW.res])
        S.out_dmas = out_dmas
        S.emit()
    return nc


_NC_CACHE = {}


def _get_nc():
    if "nc" not in _NC_CACHE:
        _NC_CACHE["nc"] = build()
    return _NC_CACHE["nc"]


def make_in_maps(inputs, cores):
    f = lambda a: np.ascontiguousarray(np.asarray(a, dtype=np.float32))
    x_prompt = f(inputs["x_prompt"]); x_sample = f(inputs["x_sample"])
    sS = f(inputs["state_hgrn_S"])[0]; sC = f(inputs["state_mlstm_C"])[0]
    sn = f(inputs["state_mlstm_n"])[0]; sm = f(inputs["state_mlstm_m"])[0]
    sconv = f(inputs["state_conv"])[0]
    cf, cb = _consts()
    lbl = f(inputs["hg_lb_logits"])
    lbl_fm = np.ascontiguousarray(lbl.reshape(2, 4, 128).transpose(2, 0, 1).reshape(128, 8))
    gcat = np.concatenate([f(inputs["hg_norm_g"])[0], f(inputs["ml_norm_g"])[0]])
    gcat_fm = np.ascontiguousarray(gcat.reshape(8, 128).T)
    bigfg = np.ascontiguousarray(np.stack([f(inputs["ml_b_ig"])[0], f(inputs["ml_b_fg"])[0]], axis=1))
    cw = f(inputs["conv_w"])[0]
    cw_fm = np.ascontiguousarray(cw.reshape(3, NFC, 128).transpose(2, 1, 0).reshape(128, NFC * 3))
    cbias_fm = np.ascontiguousarray(f(inputs["conv_b"])[0].reshape(NFC, 128).T)
    shared = {
        "w_in": f(inputs["w_in"])[0], "w_out": f(inputs["w_out"])[0], "w_gate": f(inputs["w_gate"])[0],
        "w_val": f(inputs["w_val"])[0], "w_down": f(inputs["w_down"])[0],
        "g1": f(inputs["norm1_g"]), "g2": f(inputs["norm2_g"]), "gf": f(inputs["final_norm_g"]).reshape(1, D),
        "lbl": lbl_fm, "gcat": gcat_fm, "bigfg": bigfg, "cw": cw_fm, "cbias": cbias_fm, "cf": cf, "cb": cb,
    }
    maps = []
    for c in cores:
        sl = slice(c * 16, (c + 1) * 16)
        m = dict(shared)
        m["xp"] = x_prompt[c]
        m["xs"] = np.ascontiguousarray(x_sample[sl].reshape(128, D))
        m["sS"] = np.ascontiguousarray(sS[sl])
        m["sC"] = np.ascontiguousarray(sC[sl])
        m["sn"] = np.ascontiguousarray(sn[sl].reshape(64, 128))
        m["sm"] = np.ascontiguousarray(sm[sl].T)
        m["sconv"] = np.ascontiguousarray(sconv[sl].reshape(32, DFF))
        maps.append(m)
    return maps


def assemble(results):
    n = len(results)
    y_p = np.stack([r["y_p"] for r in results])
    y_s = np.concatenate([r["y_s"].reshape(16, 8, D) for r in results])
    S_p = np.stack([r["oS_p"] for r in results])[None]
    S_s = np.concatenate([r["oS_s"] for r in results])[None]
    C_p = np.stack([r["oC_p"] for r in results])[None]
    C_s = np.concatenate([r["oC_s"] for r in results])[None]
    n_p = np.stack([r["on_p"] for r in results])[None]
    n_s = np.concatenate([r["on_s"].reshape(16, 4, 128) for r in results])[None]
    m_p = np.stack([r["om_p"].reshape(4) for r in results])[None]
    m_s = np.concatenate([r["om_s"].T for r in results])[None]
    cv_p = np.stack([r["ocv_p"] for r in results])[None]
    cv_s = np.concatenate([r["ocv_s"].reshape(16, 2, DFF) for r in results])[None]
    outs = (y_p, y_s, S_p, S_s, C_p, C_s, n_p, n_s, m_p, m_s, cv_p, cv_s)
    return tuple(np.ascontiguousarray(o, dtype=np.float32) for o in outs)


def kernel(**inputs):
    nc = _get_nc()
    maps = make_in_maps(inputs, list(range(NCORES)))
    res = run_bass_kernel_spmd(nc, maps, core_ids=list(range(NCORES)))
    return assemble(res.results)
```

```python
import contextlib
import math
import numpy as np
import ml_dtypes
import concourse.bass as bass
import concourse.mybir as mybir
from concourse.bass_utils import run_bass_kernel_spmd

F32 = mybir.dt.float32
BF16 = mybir.dt.bfloat16
AF = mybir.ActivationFunctionType
ALU = mybir.AluOpType

NCORES = 8
P = 128
D = 1024
NKC = 8
IN_COLS = 4104
DFF = 2816
NFC = 22
SEQ = 2048
NPT = 16
SAMPLE = 16
DEC_PER_CORE = 16
DEC_SEQ = 8
EPS = 1e-6
LNS = -0.5 * math.log(128.0)
BT = 2
NTB = BT * P
SBS = [[0, 1, 2, 3, 4, 5, 6, 7, SAMPLE], [8, 9, 10, 11, 12, 13, 14, 15]]
NSLOT = 9
GFF = 2
ENGS = ("pe", "act", "dve", "pool", "sp")


class Op:
    __slots__ = ("eng", "fn", "deps", "idx", "is_dma", "sig", "semval", "sem", "prev", "tag")

    def __init__(self, eng, fn, is_dma):
        self.tag = ""
        self.eng = eng
        self.fn = fn
        self.deps = []
        self.is_dma = is_dma
        self.sig = False
        self.semval = None
        self.sem = None
        self.prev = 0


class Res:
    __slots__ = ("name", "writer", "readers")

    def __init__(self, name=""):
        self.name = name
        self.writer = None
        self.readers = {}


class Sched:
    def __init__(self, nc):
        self.nc = nc
        self.streams = {e: [] for e in ENGS}
        self.n_dma_sems = {"sp": 48, "pool": 28, "act": 8}
        self.out_dmas = []
        self.fence_deps = []
        self.dmas_since_fence = []
        self.tag = ""
        self.fence_scratch = None

    def add(self, eng, fn, reads=(), writes=(), dma=False, extra=(), nofence=False):
        op = Op(eng, fn, dma)
        op.tag = self.tag
        op.idx = len(self.streams[eng])
        deps = list(self.fence_deps)
        for r in reads:
            if r.writer is not None:
                deps.append(r.writer)
        for w in writes:
            if w.writer is not None:
                deps.append(w.writer)
            deps.extend(w.readers.values())
        deps.extend(extra)
        seen = set()
        for d in deps:
            if d is op or id(d) in seen:
                continue
            seen.add(id(d))
            op.deps.append(d)
        key = ("dma", id(op)) if dma else eng
        for r in reads:
            r.readers[key] = op
        for w in writes:
            w.writer = op
            w.readers = {}
        self.streams[eng].append(op)
        if dma and not nofence:
            self.dmas_since_fence.append(op)
        return op

    def fence(self):
        deps = []
        for e in ENGS:
            for op in reversed(self.streams[e]):
                if not op.is_dma:
                    deps.append(op)
                    break
        deps.extend(self.dmas_since_fence)
        self.dmas_since_fence = []
        if self.fence_scratch is not None:
            scr = self.fence_scratch
            self.fence_deps = []
            op = self.add("pool", lambda e: e.memset(scr, 0.0), extra=deps)
            self.fence_deps = [op]
        else:
            self.fence_deps = deps

    @staticmethod
    def _near(op, d):
        return op.idx - d.idx <= 4

    def emit(self):
        nc = self.nc
        for e in ENGS:
            for op in self.streams[e]:
                for d in op.deps:
                    if d.is_dma or d.eng != op.eng:
                        d.sig = True
                    elif e != "pe" and self._near(op, d):
                        d.sig = True
        with contextlib.ExitStack() as st:
            csem = {e: st.enter_context(nc.semaphore("s_" + e)) for e in ENGS}
            dsems = {e: [st.enter_context(nc.semaphore("d_%s%d" % (e, i))) for i in range(n)]
                     for e, n in self.n_dma_sems.items()}
            for e in ENGS:
                cnt = 0
                dcnt = 0
                dvals = [0] * self.n_dma_sems.get(e, 1)
                for op in self.streams[e]:
                    if op.is_dma:
                        k = dcnt % self.n_dma_sems[e]
                        dcnt += 1
                        op.sem = dsems[e][k]
                        op.prev = dvals[k]
                        dvals[k] += 16
                        op.semval = dvals[k]
                    elif op.sig:
                        cnt += 1
                        op.sem = csem[e]
                        op.semval = cnt
            block = st.enter_context(nc.Block())

            def run(e):
                def body(eng):
                    waited = {}
                    for op in self.streams[e]:
                        needs = {}
                        for d in op.deps:
                            if d.semval is None:
                                continue
                            if (not d.is_dma) and d.eng == e and (e == "pe" or not self._near(op, d)):
                                continue
                            k = id(d.sem)
                            if needs.get(k, (None, 0))[1] < d.semval:
                                needs[k] = (d.sem, d.semval)
                        if op.is_dma and op.prev > 0:
                            k = id(op.sem)
                            if needs.get(k, (None, 0))[1] < op.prev:
                                needs[k] = (op.sem, op.prev)
                        for k, (sem, val) in needs.items():
                            if waited.get(k, 0) >= val:
                                continue
                            eng.wait_ge(sem, val)
                            waited[k] = val
                        ins = op.fn(eng)
                        if op.is_dma:
                            ins.then_inc(op.sem, 16)
                        elif op.sig:
                            ins.then_inc(op.sem, 1)
                    if e == "sp":
                        for d in self.out_dmas:
                            k = id(d.sem)
                            if waited.get(k, 0) < d.semval:
                                eng.wait_ge(d.sem, d.semval)
                                waited[k] = d.semval
                return body

            block.tensor(run("pe"))
            block.scalar(run("act"))
            block.vector(run("dve"))
            block.gpsimd(run("pool"))
            block.sync(run("sp"))
        import os
        if os.environ.get("KDUMP"):
            import json
            dump = {e: [{"tag": op.tag, "dma": op.is_dma, "deps": [[d.eng, d.idx, d.is_dma] for d in op.deps]} for op in self.streams[e]]
                    for e in ENGS}
            json.dump(dump, open(os.environ["KDUMP"], "w"))


class Buf:
    __slots__ = ("ap", "res")

    def __init__(self, ap, name=""):
        self.ap = ap
        self.res = Res(name)


class Ring:
    def __init__(self, bufs):
        self.bufs = bufs
        self.i = 0

    def get(self):
        b = self.bufs[self.i % len(self.bufs)]
        self.i += 1
        return b


CF_IDENT = 0
CF_MASK64 = 128
CF_MASK8 = 256
CF_RM64 = 384
CF_RM8 = CF_RM64 + NTB
CF_ROWMASK = CF_RM8 + 128
CF_RNEG8 = CF_ROWMASK + 16
CF_NHALF = CF_RNEG8 + 128
CF_ONE = CF_NHALF + 8
CF_ZERO = CF_ONE + 1
CF_LNS = CF_ZERO + 1
CF_EPS = CF_LNS + 1
CF_ONEROW = CF_EPS + 5
NCF = CF_ONEROW + 128


def _consts():
    cf = np.zeros((P, NCF), np.float32)
    cf[:, CF_IDENT:CF_IDENT + 128] = np.eye(128, dtype=np.float32)
    s = np.arange(128)[:, None]
    t = np.arange(128)[None, :]
    cf[:, CF_MASK64:CF_MASK64 + 128] = ((s // 64 == t // 64) & (s <= t)).astype(np.float32)
    cf[:, CF_MASK8:CF_MASK8 + 128] = ((s // 8 == t // 8) & (s <= t)).astype(np.float32)
    tt = np.arange(NTB)[None, :]
    cf[:, CF_RM64:CF_RM64 + NTB] = (tt % 64 != 0).astype(np.float32)
    cf[:, CF_RM8:CF_RM8 + 128] = (t % 8 != 0).astype(np.float32)
    cf[:, CF_ROWMASK:CF_ROWMASK + 16] = (s // 8 == np.arange(16)[None, :]).astype(np.float32)
    cf[:, CF_ONE] = 1.0
    cf[:, CF_LNS] = LNS
    cf[:, CF_EPS] = EPS
    cf[:, CF_ONEROW:CF_ONEROW + 128] = 1.0
    cf[:, CF_RNEG8:CF_RNEG8 + 128] = np.where(t % 8 == 0, -1e30, 0.0).astype(np.float32)
    cf[:, CF_NHALF:CF_NHALF + 8] = -0.5
    cb = np.concatenate([np.eye(128, dtype=np.float32), cf[:, CF_MASK64:CF_MASK64 + 128]], axis=1)
    return cf, cb.astype(ml_dtypes.bfloat16)


def build():
    nc = bass.Bass("TRN2", target_bir_lowering=False)

    def din(name, shape, dt=F32):
        return nc.dram_tensor(name, list(shape), dt, kind="ExternalInput").ap()

    def dout(name, shape):
        return nc.dram_tensor(name, list(shape), F32, kind="ExternalOutput").ap()

    xp = din("xp", [SEQ, D])
    xs = din("xs", [P, D])
    sS = din("sS", [16, 4, 128, 128])
    sC = din("sC", [16, 4, 128, 128])
    sn = din("sn", [64, 128])
    sm = din("sm", [4, 16])
    sconv = din("sconv", [32, DFF])
    w_in = din("w_in", [D, IN_COLS])
    w_out = din("w_out", [D, D])
    w_gate = din("w_gate", [D, DFF])
    w_val = din("w_val", [D, DFF])
    w_down = din("w_down", [DFF, D])
    g1 = din("g1", [1, D])
    g2 = din("g2", [1, D])
    gf = din("gf", [1, D])
    lbl = din("lbl", [P, 8])
    gcat_d = din("gcat", [P, 8])
    bigfg = din("bigfg", [4, 2])
    cw = din("cw", [P, NFC * 3])
    cbias = din("cbias", [P, NFC])
    cf_d = din("cf", [P, NCF])
    cb_d = din("cb", [P, 256], BF16)

    y_p = dout("y_p", [SEQ, D])
    y_s = dout("y_s", [P, D])
    oS_p = dout("oS_p", [4, 128, 128])
    oS_s = dout("oS_s", [16, 4, 128, 128])
    oC_p = dout("oC_p", [4, 128, 128])
    oC_s = dout("oC_s", [16, 4, 128, 128])
    on_p = dout("on_p", [4, 128])
    on_s = dout("on_s", [64, 128])
    om_p = dout("om_p", [4, 1])
    om_s = dout("om_s", [4, 16])
    ocv_p = dout("ocv_p", [2, DFF])
    ocv_s = dout("ocv_s", [32, DFF])

    S = Sched(nc)
    st = contextlib.ExitStack()
    with st:
        def sb(name, shape, dt=F32):
            t = st.enter_context(nc.sbuf_tensor(name, list(shape), dt))
            return t

        WIN = sb("WIN", [P, NKC, IN_COLS], BF16)
        WIN_res = [Res("win%d" % g) for g in range(9)]
        WOUT = sb("WOUT", [P, NKC, D], BF16)
        WOUT_res = [Res("wout%d" % k) for k in range(NKC)]
        X2 = sb("X2", [P, NSLOT, D], F32)
        X2b = [Buf(X2[:, i, :], "x2_%d" % i) for i in range(NSLOT)]
        CF = sb("CF", [P, NCF], F32)
        CFr = Res("cf")
        CB = sb("CB", [P, 256], BF16)
        CBr = Res("cb")
        GB = Buf(sb("GB", [P, D])[:, :], "gb")
        HALO = Buf(sb("HALO", [P, NFC, 2])[:, :, :], "halo")
        NROW = Buf(sb("NROW", [4, 128])[:, :], "nrow")
        SMALL = sb("SMALL", [P, 256], F32)
        SMr = Res("small")
        LB = SMALL[:, 0:4]
        OML = SMALL[:, 4:8]
        GCAT = SMALL[:, 8:16]
        LBL = SMALL[:, 16:24]
        BIGFG = SMALL[0:4, 24:26]
        NBFG = SMALL[0:4, 26:27]
        CBIAS = SMALL[:, 32:32 + NFC]
        CW = SMALL[:, 64:64 + 3 * NFC].rearrange("p (c j) -> p c j", j=3)
        SF = Buf(sb("SF", [P, 4, 128])[:, :, :], "SF")
        SBF = [Buf(sb("SBF%d" % i, [P, 4, 128], BF16)[:, :, :], "SBF%d" % i) for i in range(2)]
        CFs = Buf(sb("CFs", [P, 4, 128])[:, :, :], "CFs")
        CBFs = [Buf(sb("CBF%d" % i, [P, 4, 128], BF16)[:, :, :], "CBF%d" % i) for i in range(2)]
        NF = Buf(sb("NF", [P, 4])[:, :], "NF")
        NBF = [Buf(sb("NBF%d" % i, [P, 4], BF16)[:, :], "NBF%d" % i) for i in range(2)]
        CARRY = sb("CARRY", [4, 8], F32)
        FCAR = Buf(CARRY[:, 0:1], "fcar")
        MCAR = Buf(CARRY[:, 1:2], "mcar")
        MOUT = Buf(CARRY[:, 2:3], "mout")
        S.fence_scratch = CARRY[:, 7:8]
        ONEC = Buf(sb("ONEC", [P, 2], BF16)[:, :], "onec")

        PS = [st.enter_context(nc.psum_tensor("ps%d" % i, [P, 512], F32)) for i in range(8)]
        PSr = [Res("ps%d" % i) for i in range(8)]

        WORK_BYTES = 70912 + 4096
        WORK = sb("WORK", [P, WORK_BYTES // 4], F32)

        class Carver:
            def __init__(self):
                self.off = 0

            def take(self, shape_free, dt, name, parts=P):
                n = int(np.prod(shape_free))
                nbytes = n * (2 if dt == BF16 else 4)
                nbytes_al = (nbytes + 31) // 32 * 32
                assert self.off + nbytes_al <= WORK_BYTES, (name, self.off, nbytes_al, WORK_BYTES)
                w0 = self.off // 4
                ap = WORK[0:parts, w0:w0 + nbytes_al // 4]
                if dt == BF16:
                    ap = ap.bitcast(BF16)[:, 0:n]
                else:
                    ap = ap[:, 0:n]
                if len(shape_free) == 2:
                    ap = ap.rearrange("p (a b) -> p a b", b=shape_free[1])
                elif len(shape_free) == 3:
                    ap = ap.rearrange("p (a b c) -> p a b c", b=shape_free[1], c=shape_free[2])
                self.off += nbytes_al
                return Buf(ap, name)

        def dma(eng, out, in_, reads=(), writes=(), nofence=False, **kw):
            return S.add(eng, lambda e: e.dma_start(out=out, in_=in_, **kw), reads=reads, writes=writes, dma=True, nofence=nofence)

        def mm(out, lhsT, rhs, start, stop, reads, writes, **kw):
            kw.setdefault("skip_group_check", True)
            op = S.add("pe", lambda e: e.matmul(out, lhsT=lhsT, rhs=rhs, start=start, stop=stop, **kw),
                       reads=reads, writes=writes)
            op.tag = op.tag + "|%d*%d*%d" % (lhsT.shape[0], int(np.prod(lhsT.shape[1:])), int(np.prod(rhs.shape[1:])))
            return op

        def tr(out, in_, ident, reads, writes):
            op = S.add("pe", lambda e: e.transpose(out, in_, ident), reads=reads, writes=writes)
            op.tag = op.tag + "|%d*%d*%d" % (in_.shape[0], int(np.prod(in_.shape[1:])), in_.shape[0])
            return op

        def act(out, in_, func, reads, writes, **kw):
            return S.add("act", lambda e: e.activation(out=out, in_=in_, func=func, **kw), reads=reads, writes=writes)

        def ts(eng, out, in0, s1, s2, op0, op1, reads, writes):
            if s2 is None and eng == "pool" and op0 == ALU.mult:
                s2, op1 = 1.0, ALU.mult
            if s2 is None:
                return S.add(eng, lambda e: e.tensor_scalar(out=out, in0=in0, scalar1=s1, scalar2=None, op0=op0),
                             reads=reads, writes=writes)
            return S.add(eng, lambda e: e.tensor_scalar(out=out, in0=in0, scalar1=s1, scalar2=s2, op0=op0, op1=op1),
                         reads=reads, writes=writes)

        def tt(eng, out, in0, in1, op, reads, writes):
            return S.add(eng, lambda e: e.tensor_tensor(out=out, in0=in0, in1=in1, op=op), reads=reads, writes=writes)

        def stt(out, in0, scalar, in1, op0, op1, reads, writes):
            return S.add("dve", lambda e: e.scalar_tensor_tensor(out=out, in0=in0, scalar=scalar, in1=in1, op0=op0, op1=op1),
                         reads=reads, writes=writes)

        def cp(eng, out, in_, reads, writes):
            if eng == "act":
                return act(out, in_, AF.Copy, reads, writes)
            return S.add(eng, lambda e: e.tensor_copy(out=out, in_=in_), reads=reads, writes=writes)

        def scan(out, d0, d1, init, op0, op1, reads, writes):
            return S.add("dve", lambda e: e.tensor_tensor_scan(out=out, data0=d0, data1=d1, initial=init, op0=op0, op1=op1),
                         reads=reads, writes=writes)

        def treduce(out, in_, reads, writes):
            return S.add("dve", lambda e: e.tensor_reduce(out=out, in_=in_, axis=mybir.AxisListType.X, op=ALU.add),
                         reads=reads, writes=writes)

        def memset(eng, ap, val, writes):
            return S.add(eng, lambda e: e.memset(ap, val), writes=writes)

        IDF = CF[:, CF_IDENT:CF_IDENT + 128]
        IDB = CB[:, 0:128]
        MASKB = CB[:, 128:256]

        def psb(i, dt=F32):
            if dt == BF16:
                return PS[i][:, :].bitcast(BF16)
            return PS[i][:, :]

        dma("sp", CF[:, :], cf_d[:, :], writes=[CFr])
        dma("sp", CB[:, :], cb_d[:, :], writes=[CBr])
        dma("sp", LBL, lbl[:, :], writes=[SMr])
        dma("sp", GCAT, gcat_d[:, :], writes=[SMr])
        dma("sp", BIGFG, bigfg[:, :], writes=[SMr])
        dma("sp", CBIAS, cbias[:, :], writes=[SMr])
        dma("sp", SMALL[:, 64:64 + 3 * NFC], cw[:, :], writes=[SMr])
        w_in_r = w_in.rearrange("(kc p) f -> p kc f", p=P)
        for gs in ((0, 1), (8,), (4, 5), (2, 3), (6, 7)):
            c0, c1 = (gs[0] * 512, (gs[-1] + 1) * 512) if gs[0] < 8 else (4096, 4104)
            for g in gs[1:]:
                WIN_res[g] = WIN_res[gs[0]]
            dma("pool", WIN[:, :, c0:c1], w_in_r[:, :, c0:c1], writes=[WIN_res[gs[0]]])
        tt("dve", LB, LBL[:, 0:4], LBL[:, 4:8], ALU.subtract, reads=[SMr], writes=[SMr])
        act(LB, LB, AF.Sigmoid, reads=[SMr], writes=[SMr])
        ts("dve", OML, LB, -1.0, 1.0, ALU.mult, ALU.add, reads=[SMr], writes=[SMr])
        ts("dve", NBFG, BIGFG[:, 1:2], -1.0, None, ALU.mult, None, reads=[SMr], writes=[SMr])
        memset("dve", ONEC.ap, 1.0, writes=[ONEC.res])
        memset("pool", SF.ap, 0.0, writes=[SF.res])
        memset("pool", SBF[0].ap, 0.0, writes=[SBF[0].res])
        memset("pool", CFs.ap, 0.0, writes=[CFs.res])
        memset("pool", CBFs[0].ap, 0.0, writes=[CBFs[0].res])
        memset("pool", NF.ap, 0.0, writes=[NF.res])
        memset("pool", NBF[0].ap, 0.0, writes=[NBF[0].res])
        memset("pool", CARRY[:, :], 0.0, writes=[FCAR.res, MCAR.res, MOUT.res])

        out_dmas = []

        def odma(out, in_, reads, **kw):
            op = dma("sp", out, in_, reads=reads, **kw)
            out_dmas.append(op)
            return op

        def phase1(tiles, slots, first):
            cv = Carver()
            HBF = Ring([cv.take([D], BF16, "hbf%d" % i) for i in range(2)])
            JUNK = cv.take([128], BF16, "junk")
            TF = Ring([cv.take([NTB], F32, "tf%d" % i) for i in range(4)])
            RSB = [cv.take([NTB], F32, "rs%d" % i, parts=4) for i in range(6)]
            RSM = [cv.take([16], F32, "rsm%d" % i, parts=4) for i in range(3)] + [cv.take([64], F32, "rsm3", parts=4)]
            VHG = Ring([cv.take([512], BF16, "vhg%d" % i) for i in range(2)])
            SGT = Ring([cv.take([512], BF16, "sgt%d" % i) for i in range(2)])
            KPP = Ring([cv.take([4, 128], BF16, "kpp%d" % i) for i in range(2)])
            VML = Ring([cv.take([512], BF16, "vml%d" % i) for i in range(2)])
            SO = Ring([cv.take([512], BF16, "so%d" % i) for i in range(2)])
            TSG = Ring([cv.take([512], F32, "tsg%d" % i) for i in range(1)])
            ATB = Ring([cv.take([4, 128], BF16, "atb%d" % i) for i in range(2)])
            KHTOK = cv.take([4, 128], BF16, "khtok")
            OCAT = cv.take([D], BF16, "ocat")
            OT = cv.take([NKC, 128], BF16, "oT")
            SSQ = Ring([cv.take([8], F32, "ssq%d" % i) for i in range(6)])
            XSS = Ring([cv.take([2], F32, "xss%d" % i) for i in range(4)])
            EB = cv.take([4, NTB], F32, "eb")
            EV = Ring([cv.take([NTB], F32, "ev%d" % i) for i in range(2)])
            QS = cv.take([NTB], F32, "qs")
            EBr = [Res("eb%d" % h) for h in range(4)]

            SM0 = cv.take([16], F32, "sm0", parts=4)

            def make_set(tag):
                d = {}
                d["HT"] = cv.take([NKC, NTB], BF16, "hT" + tag)
                d["HTr"] = [[Res("hT%s_%d_%d" % (tag, t, pk)) for pk in range(2)] for t in range(BT)]
                for nm in ("QT", "KT", "KH", "MQ", "MK"):
                    d[nm] = cv.take([4, NTB], BF16, nm + tag)
                    d[nm + "r"] = [Res("%s%s%d" % (nm, tag, h)) for h in range(4)]
                d["ELAST"] = cv.take([4, 16], F32, "elast" + tag)
                d["ELr"] = [Res("el%s%d" % (tag, h)) for h in range(4)]
                d["TSC"] = [cv.take([12], F32, "tsc%s%d" % (tag, i)) for i in range(BT)]
                d["FEB"] = cv.take([4, 16], F32, "feb" + tag)
                return d
            sets = [make_set("a")]
            mark = cv.off
            SSF = Ring([cv.take([4, 128], F32, "ssf%d" % i) for i in range(3)])
            SSB = Ring([cv.take([4, 128], BF16, "ssb%d" % i) for i in range(2)])
            SSN = Ring([cv.take([4, 128], F32, "ssn%d" % i) for i in range(2)])
            QM = Ring([cv.take([4, 128], BF16, "qm%d" % i) for i in range(2)])
            KM = Ring([cv.take([4, 128], BF16, "km%d" % i) for i in range(2)])
            SN_ROW = cv.take([128], F32, "snrow")
            SN_F = cv.take([64], F32, "snf")
            SN_B = cv.take([64], BF16, "snb")
            SN_NEW = cv.take([64], F32, "snnew")
            SN_QN = cv.take([64], F32, "snqn")
            end_s = cv.off
            cv.off = mark
            sets.append(make_set("b"))
            cv.off = max(cv.off, end_s)

            dma("sp", GB.ap, g1[0, :].partition_broadcast(P), writes=[GB.res])
            def load_wout():
                for kc in range(NKC):
                    wb = X2b[2 + (kc % 7)]
                    dma("act", wb.ap, w_out[kc * P:(kc + 1) * P, :], writes=[wb.res])
                    ts("pool", WOUT[:, kc, :], wb.ap, GCAT[:, kc:kc + 1], None, ALU.mult, None,
                       reads=[wb.res, SMr], writes=[WOUT_res[kc]])

            PBIG = Ring([0, 1, 2])
            PTR = 3
            PA = Ring([4, 5])
            PO_H, PO_M = 6, 7

            blocks = []
            i = 0
            while i < len(tiles):
                if tiles[i] == SAMPLE:
                    blocks.append([i])
                    i += 1
                else:
                    blocks.append(list(range(i, min(i + BT, len(tiles)))))
                    i += len(blocks[-1])

            def lab(gen, label):
                stepc = [0]
                while True:
                    S.tag = "%s.%d" % (label, stepc[0])
                    try:
                        next(gen)
                    except StopIteration:
                        return
                    stepc[0] += 1
                    yield

            def blk_params(blk):
                is_s = tiles[blk[0]] == SAMPLE
                nt = len(blk)
                NT = nt * P
                L = 8 if is_s else 64
                return is_s, nt, NT, L, NT // L

            HBS = {}

            def gen_Ac(blk, bs, key):
                is_s, nt, NT, L, NBb = blk_params(blk)
                for bi, ti in enumerate(blk):
                    tile = tiles[ti]
                    xb = X2b[slots[ti]]
                    src = xs[:, :] if is_s else xp[tile * P:(tile + 1) * P, :]
                    dma("sp", xb.ap, src, writes=[xb.res])
                    ss = XSS.get()
                    hb = HBF.get()
                    act(hb.ap, xb.ap, AF.Square, reads=[xb.res], writes=[hb.res, ss.res], accum_out=ss.ap[:, 0:1])
                    ts("dve", ss.ap[:, 1:2], ss.ap[:, 0:1], 1.0 / D, EPS, ALU.mult, ALU.add, reads=[ss.res], writes=[ss.res])
                    tt("pool", ss.ap[:, 1:2], ss.ap[:, 1:2], CF[:, CF_NHALF:CF_NHALF + 1], ALU.pow, reads=[ss.res, CFr], writes=[ss.res])
                    stt(hb.ap, xb.ap, ss.ap[:, 1:2], GB.ap, ALU.mult, ALU.mult, reads=[xb.res, ss.res, GB.res], writes=[hb.res])
                    HBS[(key, bi)] = hb
                    yield

            def gen_At(blk, bs, key):
                hT = bs["HT"]
                for bi, ti in enumerate(blk):
                    hb = HBS.pop((key, bi))
                    for pk in range(2):
                        tb_ = PBIG.get()
                        pt = psb(tb_, BF16)[:, 0:512].rearrange("p (k t) -> p k t", t=128)
                        for k4 in range(4):
                            kc = pk * 4 + k4
                            tr(pt[:, k4, :], hb.ap[:, kc * P:(kc + 1) * P], IDB, reads=[hb.res, CBr], writes=[PSr[tb_]])
                        cp("act", hT.ap[:, pk * 4:(pk + 1) * 4, bi * P:(bi + 1) * P], pt, reads=[PSr[tb_]], writes=[bs["HTr"][bi][pk]])
                        yield

            def gen_B1(blk, bs):
                is_s, nt, NT, L, NBb = blk_params(blk)
                hT = bs["HT"]
                hTr = [r for bi in range(nt) for r in bs["HTr"][bi]]
                RM = CF[:, CF_RM8:CF_RM8 + 128] if is_s else CF[:, CF_RM64:CF_RM64 + NT]
                QT, KT, KH, MQ, MK = bs["QT"], bs["KT"], bs["KH"], bs["MQ"], bs["MK"]
                QTr, KTr, KHr, MQr, MKr = bs["QTr"], bs["KTr"], bs["KHr"], bs["MQr"], bs["MKr"]
                ELAST, ELr, TSC, FEB = bs["ELAST"], bs["ELr"], bs["TSC"], bs["FEB"]

                def fm_proj(col0, ncols, bank, wres):
                    out = PS[bank][0:ncols, 0:NT]
                    for kc in range(NKC):
                        mm(out, WIN[:, kc, col0:col0 + ncols], hT.ap[:, kc, 0:NT], kc == 0, kc == NKC - 1,
                           reads=[wres] + hTr, writes=[PSr[bank]])
                    return out

                bk_i = PBIG.get()
                p_ig = fm_proj(4096, 4, bk_i, WIN_res[8])
                R_ig = RSB[0]
                ts("dve", R_ig.ap[:, 0:NT], p_ig, BIGFG[:, 0:1], None, ALU.add, None, reads=[PSr[bk_i], SMr], writes=[R_ig.res])
                bk_f = PBIG.get()
                p_fg = fm_proj(4100, 4, bk_f, WIN_res[8])
                R_e = RSB[1]
                act(R_e.ap[:, 0:NT], p_fg, AF.Exp, reads=[PSr[bk_f], SMr], writes=[R_e.res], scale=-1.0, bias=NBFG)
                act(R_e.ap[:, 0:NT], R_e.ap[:, 0:NT], AF.Ln, reads=[R_e.res, CFr], writes=[R_e.res], bias=CF[0:4, CF_ONE:CF_ONE + 1])
                R_F = RSB[2]
                if is_s:
                    scan(R_F.ap[:, 0:NT], CF[0:4, CF_RM8:CF_RM8 + 128], R_e.ap[:, 0:NT], 0.0, ALU.mult, ALU.subtract,
                         reads=[CFr, R_e.res], writes=[R_F.res])
                else:
                    scan(R_F.ap[:, 0:NT], CF[0:4, CF_ONE:CF_ONE + 1].to_broadcast([4, NT]), R_e.ap[:, 0:NT], 0.0, ALU.mult, ALU.subtract,
                         reads=[CFr, R_e.res], writes=[R_F.res])
                R_G = RSB[3]
                tt("dve", R_G.ap[:, 0:NT], R_ig.ap[:, 0:NT], R_F.ap[:, 0:NT], ALU.subtract, reads=[R_ig.res, R_F.res], writes=[R_G.res])
                R_M = RSB[4]
                R_Mc = RSM[0]
                if is_s:
                    dma("sp", SM0.ap, sm[:, :], writes=[SM0.res])
                    R_G2 = RSB[5]
                    cp("dve", R_G2.ap[:, 0:NT], R_G.ap[:, 0:NT], reads=[R_G.res], writes=[R_G2.res])
                    g23 = R_G2.ap[:, 0:NT].rearrange("p (j t) -> p j t", t=8)
                    tt("dve", g23[:, :, 0], g23[:, :, 0], SM0.ap, ALU.max, reads=[R_G2.res, SM0.res], writes=[R_G2.res])
                    scan(R_M.ap[:, 0:NT], CF[0:4, CF_RNEG8:CF_RNEG8 + 128], R_G2.ap[:, 0:NT], 0.0, ALU.add, ALU.max,
                         reads=[CFr, R_G2.res], writes=[R_M.res])
                    cp("dve", R_Mc.ap[:, 0:NBb], SM0.ap, reads=[SM0.res], writes=[R_Mc.res])
                else:
                    scan(R_M.ap[:, 0:NT], CF[0:4, CF_ZERO:CF_ZERO + 1].to_broadcast([4, NT]), R_G.ap[:, 0:NT], MCAR.ap, ALU.add, ALU.max,
                         reads=[CFr, R_G.res, MCAR.res], writes=[R_M.res])
                    m3 = R_M.ap[:, 0:NT].rearrange("p (j t) -> p j t", t=L)
                    cp("dve", R_Mc.ap[:, 0:1], MCAR.ap, reads=[MCAR.res], writes=[R_Mc.res])
                    cp("dve", R_Mc.ap[:, 1:NBb], m3[:, 0:NBb - 1, L - 1], reads=[R_M.res], writes=[R_Mc.res])
                m3 = R_M.ap[:, 0:NT].rearrange("p (j t) -> p j t", t=L)
                f3 = R_F.ap[:, 0:NT].rearrange("p (j t) -> p j t", t=L)
                g3 = R_G.ap[:, 0:NT].rearrange("p (j t) -> p j t", t=L)
                Mc_b = R_Mc.ap[:, 0:NBb].unsqueeze(2).to_broadcast([4, NBb, L])
                Me_b = m3[:, :, L - 1:L].to_broadcast([4, NBb, L])
                R_fe = RSM[1]
                tt("dve", R_fe.ap[:, 0:NBb], R_Mc.ap[:, 0:NBb], m3[:, :, L - 1], ALU.subtract, reads=[R_Mc.res, R_M.res], writes=[R_fe.res])
                act(R_fe.ap[:, 0:NBb], R_fe.ap[:, 0:NBb], AF.Exp, reads=[R_fe.res], writes=[R_fe.res])
                R_es = RSB[0]
                es3 = R_es.ap[:, 0:NT].rearrange("p (j t) -> p j t", t=L)
                tt("dve", es3, g3, Mc_b, ALU.subtract, reads=[R_G.res, R_Mc.res], writes=[R_es.res])
                ts("dve", R_es.ap[:, 0:NT], R_es.ap[:, 0:NT], 60.0, LNS, ALU.min, ALU.add, reads=[R_es.res], writes=[R_es.res])
                act(R_es.ap[:, 0:NT], R_es.ap[:, 0:NT], AF.Exp, reads=[R_es.res], writes=[R_es.res])
                R_we = RSB[1]
                we3 = R_we.ap[:, 0:NT].rearrange("p (j t) -> p j t", t=L)
                tt("dve", we3, g3, Me_b, ALU.subtract, reads=[R_G.res, R_M.res], writes=[R_we.res])
                act(R_we.ap[:, 0:NT], R_we.ap[:, 0:NT], AF.Exp, reads=[R_we.res, CFr], writes=[R_we.res], bias=CF[0:4, CF_LNS:CF_LNS + 1])
                R_fl = RSB[5]
                fl3 = R_fl.ap[:, 0:NT].rearrange("p (j t) -> p j t", t=L)
                tt("dve", fl3, f3, Mc_b, ALU.add, reads=[R_F.res, R_Mc.res], writes=[R_fl.res])
                act(R_fl.ap[:, 0:NT], R_fl.ap[:, 0:NT], AF.Exp, reads=[R_fl.res], writes=[R_fl.res], scale=-1.0)
                if is_s:
                    R_mo = RSM[2]
                    tt("dve", R_mo.ap[:, 0:16], f3[:, :, L - 1], m3[:, :, L - 1], ALU.add, reads=[R_F.res, R_M.res], writes=[R_mo.res])
                    odma(om_s[:, :], R_mo.ap[:, 0:16], reads=[R_mo.res])
                else:
                    tt("dve", MOUT.ap, R_F.ap[:, NT - 1:NT], R_M.ap[:, NT - 1:NT], ALU.add, reads=[R_F.res, R_M.res], writes=[MOUT.res])
                    cp("dve", MCAR.ap, MOUT.ap, reads=[MOUT.res], writes=[MCAR.res])
                R_fd = RSM[3]
                fd3 = R_fd.ap[:, 0:4 * NBb].rearrange("p (h j) -> p h j", j=NBb)
                tt("dve", fd3, R_fe.ap[:, 0:NBb].unsqueeze(1).to_broadcast([4, 4, NBb]),
                   CF[0:4, CF_IDENT:CF_IDENT + 4].unsqueeze(2).to_broadcast([4, 4, NBb]), ALU.mult,
                   reads=[R_fe.res, CFr], writes=[R_fd.res])
                yield

                def hf_evac(h):
                    bk = PBIG.get()
                    p_f = fm_proj(512 + h * 128, 128, bk, WIN_res[1])
                    E = EV.get()
                    act(E.ap[:, 0:NT], p_f, AF.Exp, reads=[PSr[bk]], writes=[E.res], scale=-1.0)
                    return E

                def hf_chain(h, E):
                    A = TF.get()
                    act(A.ap[:, 0:NT], E.ap[:, 0:NT], AF.Ln, reads=[E.res, SMr, CFr], writes=[A.res],
                        scale=LB[:, h:h + 1], bias=CF[:, CF_ONE:CF_ONE + 1])
                    C = TF.get()
                    act(C.ap[:, 0:NT], E.ap[:, 0:NT], AF.Ln, reads=[E.res, CFr], writes=[C.res], bias=CF[:, CF_ONE:CF_ONE + 1])
                    tt("dve", A.ap[:, 0:NT], A.ap[:, 0:NT], C.ap[:, 0:NT], ALU.subtract, reads=[A.res, C.res], writes=[A.res])
                    T4 = TF.get()
                    scan(T4.ap[:, 0:NT], RM, A.ap[:, 0:NT], 0.0, ALU.mult, ALU.add, reads=[CFr, A.res], writes=[T4.res])
                    b3 = T4.ap[:, 0:NT].rearrange("p (j t) -> p j t", t=L)
                    act(A.ap[:, 0:NT], A.ap[:, 0:NT], AF.Exp, reads=[A.res], writes=[A.res])
                    ts("pool", A.ap[:, 0:NT], A.ap[:, 0:NT], -1.0, 1.0, ALU.mult, ALU.add, reads=[A.res], writes=[A.res])
                    act(C.ap[:, 0:NT], T4.ap[:, 0:NT], AF.Exp, reads=[T4.res, C.res], writes=[C.res], scale=-1.0)
                    tt("pool", KT.ap[:, h, 0:NT], A.ap[:, 0:NT], C.ap[:, 0:NT], ALU.mult, reads=[A.res, C.res], writes=[KTr[h]])
                    T7 = TF.get()
                    t73 = T7.ap[:, 0:NT].rearrange("p (j t) -> p j t", t=L)
                    tt("dve", t73, b3[:, :, L - 1:L].to_broadcast([P, NBb, L]), b3, ALU.subtract, reads=[T4.res], writes=[T7.res])
                    act(T7.ap[:, 0:NT], T7.ap[:, 0:NT], AF.Exp, reads=[T7.res], writes=[T7.res])
                    tt("pool", KH.ap[:, h, 0:NT], A.ap[:, 0:NT], T7.ap[:, 0:NT], ALU.mult, reads=[A.res, T7.res], writes=[KHr[h]])
                    act(ELAST.ap[:, h, 0:NBb], b3[:, :, L - 1], AF.Exp, reads=[T4.res], writes=[ELr[h]])
                    act(EB.ap[:, h, 0:NT], T4.ap[:, 0:NT], AF.Exp, reads=[T4.res], writes=[EBr[h]])
                for h in range(4):
                    bk = PBIG.get()
                    p = fm_proj(2048 + h * 128, 128, bk, WIN_res[4])
                    cp("act", MQ.ap[:, h, 0:NT], p, reads=[PSr[bk]], writes=[MQr[h]])
                    yield
                    bk = PBIG.get()
                    p = fm_proj(2560 + h * 128, 128, bk, WIN_res[5])
                    cp("act", MK.ap[:, h, 0:NT], p, reads=[PSr[bk]], writes=[MKr[h]])
                    yield
                def hq_unit(h):
                    bq = PBIG.get()
                    p_q = fm_proj(h * 128, 128, bq, WIN_res[0])
                    qs = QS
                    cp("act", qs.ap[:, 0:NT], p_q, reads=[PSr[bq]], writes=[qs.res])
                    tt("pool", QT.ap[:, h, 0:NT], qs.ap[:, 0:NT], EB.ap[:, h, 0:NT], ALU.mult, reads=[qs.res, EBr[h]], writes=[QTr[h]])
                Es = {0: hf_evac(0)}
                yield
                for h in range(4):
                    if h + 1 < 4:
                        Es[h + 1] = hf_evac(h + 1)
                    if h >= 1:
                        hq_unit(h - 1)
                        yield
                    hf_chain(h, Es.pop(h))
                    yield
                hq_unit(3)
                yield
                ptfs = [PS[PTR][:, 256:268], PS[PTR][:, 468:480]]
                for bi in range(nt):
                    for qi, R_q in enumerate((R_es, R_we, R_fl)):
                        tr(ptfs[bi][:, qi * 4:(qi + 1) * 4], R_q.ap[:, bi * P:(bi + 1) * P], CF[0:4, CF_IDENT:CF_IDENT + 4],
                           reads=[R_q.res, CFr], writes=[PSr[PTR]])
                pfe = PS[PTR][:, 276:276 + 4 * NBb]
                mm(pfe, CF[0:4, CF_ONEROW:CF_ONEROW + 128], R_fd.ap[:, 0:4 * NBb], True, True,
                   reads=[CFr, R_fd.res], writes=[PSr[PTR]])
                for bi in range(nt):
                    cp("dve", TSC[bi].ap, ptfs[bi], reads=[PSr[PTR]], writes=[TSC[bi].res])
                cp("dve", FEB.ap[:, :, 0:NBb], pfe.rearrange("p (h j) -> p h j", j=NBb), reads=[PSr[PTR]], writes=[FEB.res])
                yield

            def gen_B2(blk, bs, bi, out):
                hT = bs["HT"]
                tok = slice(bi * P, (bi + 1) * P)
                hTr = bs["HTr"][bi]
                tsc = bs["TSC"][bi]

                def tm_proj(g):
                    bank = PBIG.get()
                    for kc in range(NKC):
                        mm(PS[bank][:, :], hT.ap[:, kc, tok], WIN[:, kc, g * 512:(g + 1) * 512], kc == 0, kc == NKC - 1,
                           reads=[WIN_res[g]] + hTr, writes=[PSr[bank]])
                    return bank
                bk = tm_proj(2)
                vhg = VHG.get()
                cp("dve", vhg.ap, PS[bk][:, :], reads=[PSr[bk]], writes=[vhg.res])
                out["vhg"] = vhg
                yield
                bk = tm_proj(3)
                tsg = TSG.get()
                act(tsg.ap, PS[bk][:, :], AF.Sigmoid, reads=[PSr[bk]], writes=[tsg.res])
                sgt = SGT.get()
                tt("dve", sgt.ap, PS[bk][:, :], tsg.ap, ALU.mult, reads=[PSr[bk], tsg.res], writes=[sgt.res])
                out["sgt"] = sgt
                yield
                bk = tm_proj(6)
                vml = VML.get()
                cp("dve", vml.ap, PS[bk][:, :], reads=[PSr[bk]], writes=[vml.res])
                out["vml"] = vml
                yield
                bk = tm_proj(7)
                so = SO.get()
                act(so.ap, PS[bk][:, :], AF.Sigmoid, reads=[PSr[bk]], writes=[so.res])
                out["so"] = so
                yield
                bk = tm_proj(5)
                kpp = KPP.get()
                for h in range(4):
                    act(kpp.ap[:, h, :], PS[bk][:, h * 128:(h + 1) * 128], AF.Copy, reads=[PSr[bk], tsc.res], writes=[kpp.res],
                        scale=tsc.ap[:, 4 + h:5 + h])
                out["kpp"] = kpp
                yield

            def gen_R(blk, bs, bi, tb):
                ti = blk[bi]
                xb = X2b[slots[ti]]
                tok = slice(bi * P, (bi + 1) * P)
                tsc = bs["TSC"][bi]
                QT, KT, KH, MQ, MK = bs["QT"], bs["KT"], bs["KH"], bs["MQ"], bs["MK"]
                QTr, KTr, KHr, MQr, MKr = bs["QTr"], bs["KTr"], bs["KHr"], bs["MQr"], bs["MKr"]
                ELAST, ELr, FEB = bs["ELAST"], bs["ELr"], bs["FEB"]
                MASK = CF[:, CF_MASK64:CF_MASK64 + 128]
                vhg, sgt, kpp, vml, so = tb["vhg"], tb["sgt"], tb["kpp"], tb["vml"], tb["so"]
                tb_ = PBIG.get()
                ptk = psb(tb_, BF16)[:, 0:512].rearrange("p (h k) -> p h k", k=128)
                for h in range(4):
                    tr(ptk[:, h, :], KH.ap[:, h, tok], IDB, reads=[KHr[h], CBr], writes=[PSr[tb_]])
                cp("act", KHTOK.ap, ptk, reads=[PSr[tb_]], writes=[KHTOK.res])
                pa = PA.get()
                pa3 = PS[pa][:, :].rearrange("p (h t) -> p h t", t=128)
                for h in range(4):
                    mm(pa3[:, h, :], KT.ap[:, h, tok], QT.ap[:, h, tok], True, True, reads=[KTr[h], QTr[h]], writes=[PSr[pa]])
                atbh = ATB.get()
                tt("dve", atbh.ap, pa3, MASK.unsqueeze(1).to_broadcast([P, 4, 128]), ALU.mult, reads=[PSr[pa], CFr], writes=[atbh.res])
                pa = PA.get()
                pa3 = PS[pa][:, :].rearrange("p (h t) -> p h t", t=128)
                for h in range(4):
                    mm(pa3[:, h, :], MK.ap[:, h, tok], MQ.ap[:, h, tok], True, True, reads=[MKr[h], MQr[h]], writes=[PSr[pa]])
                atbm = ATB.get()
                for h in range(4):
                    act(atbm.ap[:, h, :], pa3[:, h, :], AF.Copy, reads=[PSr[pa], tsc.res], writes=[atbm.res], scale=tsc.ap[:, h:h + 1])
                tt("pool", atbm.ap, atbm.ap, MASKB.unsqueeze(1).to_broadcast([P, 4, 128]), ALU.mult, reads=[atbm.res, CBr], writes=[atbm.res])
                yield

                def hg_update(j, dst_bf):
                    bank = PO_H if j == 0 else PA.get()
                    pu = PS[bank][:, :].rearrange("p (h v) -> p h v", v=128)
                    rows = slice(j * 64, (j + 1) * 64)
                    for h in range(4):
                        mm(pu[:, h, :], KHTOK.ap[rows, h, :], vhg.ap[rows, h * 128:(h + 1) * 128], h == 0, h == 3,
                           reads=[KHTOK.res, vhg.res], writes=[PSr[bank]])
                    for h in range(4):
                        stt(SF.ap[:, h, :], SF.ap[:, h, :], ELAST.ap[:, h, bi * 2 + j:bi * 2 + j + 1], pu[:, h, :], ALU.mult, ALU.add,
                            reads=[SF.res, ELr[h], PSr[bank]], writes=[SF.res])
                    cp("dve", dst_bf.ap, SF.ap, reads=[SF.res], writes=[dst_bf.res])

                def ml_update(j, dst_c, dst_n):
                    bank = PO_M if j == 0 else PA.get()
                    pu = PS[bank][:, :].rearrange("p (h v) -> p h v", v=128)
                    rows = slice(j * 64, (j + 1) * 64)
                    for h in range(4):
                        mm(pu[:, h, :], kpp.ap[rows, h, :], vml.ap[rows, h * 128:(h + 1) * 128], h == 0, h == 3,
                           reads=[kpp.res, vml.res], writes=[PSr[bank]])
                    pnn = PS[PTR][:, 272:276]
                    for h in range(4):
                        mm(pnn[:, h:h + 1], kpp.ap[rows, h, :], ONEC.ap[rows, 0:1], True, True,
                           reads=[kpp.res, ONEC.res], writes=[PSr[PTR]])
                    jj = bi * 2 + j
                    for h in range(4):
                        stt(CFs.ap[:, h, :], CFs.ap[:, h, :], FEB.ap[:, h, jj:jj + 1], pu[:, h, :], ALU.mult, ALU.add,
                            reads=[CFs.res, FEB.res, PSr[bank]], writes=[CFs.res])
                    cp("dve", dst_c.ap, CFs.ap, reads=[CFs.res], writes=[dst_c.res])
                    tt("dve", NF.ap, NF.ap, FEB.ap[:, :, jj], ALU.mult, reads=[NF.res, FEB.res], writes=[NF.res])
                    tt("dve", NF.ap, NF.ap, pnn, ALU.add, reads=[NF.res, PSr[PTR]], writes=[NF.res])
                    cp("dve", dst_n.ap, NF.ap, reads=[NF.res], writes=[dst_n.res])
                hg_update(0, SBF[1])
                ml_update(0, CBFs[1], NBF[1])
                yield
                po3 = PS[PO_H][:, :].rearrange("p (h v) -> p h v", v=128)
                for h in range(4):
                    mm(po3[:, h, :], atbh.ap[:, h, :], vhg.ap[:, h * 128:(h + 1) * 128], h == 0, False,
                       reads=[atbh.res, vhg.res], writes=[PSr[PO_H]])
                    mm(po3[0:64, h, :], QT.ap[:, h, bi * P:bi * P + 64], SBF[0].ap[:, h, :], False, True,
                       reads=[QTr[h], SBF[0].res], writes=[PSr[PO_H]])
                    mm(po3[64:128, h, :], QT.ap[:, h, bi * P + 64:bi * P + 128], SBF[1].ap[:, h, :], False, True,
                       reads=[QTr[h], SBF[1].res], writes=[PSr[PO_H]])
                sq = SSQ.get()
                for h in range(4):
                    act(JUNK.ap[:, 0:128], po3[:, h, :], AF.Square, reads=[PSr[PO_H]], writes=[JUNK.res, sq.res],
                        accum_out=sq.ap[:, h:h + 1])
                ts("dve", sq.ap[:, 4:8], sq.ap[:, 0:4], 1.0 / 128, EPS, ALU.mult, ALU.add, reads=[sq.res], writes=[sq.res])
                tt("pool", sq.ap[:, 4:8], sq.ap[:, 4:8], CF[:, CF_NHALF:CF_NHALF + 4], ALU.pow, reads=[sq.res, CFr], writes=[sq.res])
                for h in range(4):
                    stt(OCAT.ap[:, h * 128:(h + 1) * 128], po3[:, h, :], sq.ap[:, 4 + h:5 + h], sgt.ap[:, h * 128:(h + 1) * 128],
                        ALU.mult, ALU.mult, reads=[PSr[PO_H], sq.res, sgt.res], writes=[OCAT.res])
                yield
                pn3 = PS[PO_M][:, :].rearrange("p (h v) -> p h v", v=128)
                pd = PS[PTR][:, 268:272]
                for h in range(4):
                    mm(pn3[:, h, :], atbm.ap[:, h, :], vml.ap[:, h * 128:(h + 1) * 128], h == 0, False,
                       reads=[atbm.res, vml.res], writes=[PSr[PO_M]])
                    mm(pn3[0:64, h, :], MQ.ap[:, h, bi * P:bi * P + 64], CBFs[0].ap[:, h, :], False, True,
                       reads=[MQr[h], CBFs[0].res], writes=[PSr[PO_M]])
                    mm(pn3[64:128, h, :], MQ.ap[:, h, bi * P + 64:bi * P + 128], CBFs[1].ap[:, h, :], False, True,
                       reads=[MQr[h], CBFs[1].res], writes=[PSr[PO_M]])
                for h in range(4):
                    mm(pd[:, h:h + 1], atbm.ap[:, h, :], ONEC.ap[:, 0:1], True, False, reads=[atbm.res, ONEC.res], writes=[PSr[PTR]])
                    mm(pd[0:64, h:h + 1], MQ.ap[:, h, bi * P:bi * P + 64], NBF[0].ap[:, h:h + 1], False, True,
                       reads=[MQr[h], NBF[0].res], writes=[PSr[PTR]])
                    mm(pd[64:128, h:h + 1], MQ.ap[:, h, bi * P + 64:bi * P + 128], NBF[1].ap[:, h:h + 1], False, True,
                       reads=[MQr[h], NBF[1].res], writes=[PSr[PTR]])
                ml_norm(pn3, pd, PSr[PTR], tsc, so)
                yield
                hg_update(1, SBF[0])
                ml_update(1, CBFs[0], NBF[0])
                yield
                for u in out_proj(xb):
                    yield

            def ml_norm(pn3, pd, pd_res, tsc, so):
                sq = SSQ.get()
                for h in range(4):
                    act(JUNK.ap[:, 0:128], pn3[:, h, :], AF.Square, reads=[PSr[PO_M]], writes=[JUNK.res, sq.res],
                        accum_out=sq.ap[:, h:h + 1])
                sq2 = SSQ.get()
                ts("dve", sq2.ap[:, 4:8], pd, -1.0, None, ALU.mult, None, reads=[pd_res], writes=[sq2.res])
                tt("dve", sq2.ap[:, 0:4], pd, sq2.ap[:, 4:8], ALU.max, reads=[pd_res, sq2.res], writes=[sq2.res])
                tt("dve", sq2.ap[:, 0:4], sq2.ap[:, 0:4], tsc.ap[:, 8:12], ALU.max, reads=[sq2.res, tsc.res], writes=[sq2.res])
                tt("dve", sq2.ap[:, 0:4], sq2.ap[:, 0:4], sq2.ap[:, 0:4], ALU.mult, reads=[sq2.res], writes=[sq2.res])
                ts("dve", sq.ap[:, 4:8], sq.ap[:, 0:4], 1.0 / 128, None, ALU.mult, None, reads=[sq.res], writes=[sq.res])
                stt(sq.ap[:, 4:8], sq2.ap[:, 0:4], EPS, sq.ap[:, 4:8], ALU.mult, ALU.add, reads=[sq.res, sq2.res], writes=[sq.res])
                tt("pool", sq.ap[:, 4:8], sq.ap[:, 4:8], CF[:, CF_NHALF:CF_NHALF + 4], ALU.pow, reads=[sq.res, CFr], writes=[sq.res])
                for h in range(4):
                    stt(OCAT.ap[:, 512 + h * 128:512 + (h + 1) * 128], pn3[:, h, :], sq.ap[:, 4 + h:5 + h], so.ap[:, h * 128:(h + 1) * 128],
                        ALU.mult, ALU.mult, reads=[PSr[PO_M], sq.res, so.res], writes=[OCAT.res])

            def out_proj(xb):
                for pk in range(2):
                    tb_ = PBIG.get()
                    pt = psb(tb_, BF16)[:, 0:512].rearrange("p (k t) -> p k t", t=128)
                    for k4 in range(4):
                        kc = pk * 4 + k4
                        tr(pt[:, k4, :], OCAT.ap[:, kc * P:(kc + 1) * P], IDB, reads=[OCAT.res, CBr], writes=[PSr[tb_]])
                    cp("act", OT.ap[:, pk * 4:(pk + 1) * 4, :], pt, reads=[PSr[tb_]], writes=[OT.res])
                yield
                for half in range(2):
                    bank = PBIG.get()
                    for kc in range(NKC):
                        mm(PS[bank][:, :], OT.ap[:, kc, :], WOUT[:, kc, half * 512:(half + 1) * 512], kc == 0, kc == NKC - 1,
                           reads=[OT.res, WOUT_res[kc]], writes=[PSr[bank]])
                    tt("dve", xb.ap[:, half * 512:(half + 1) * 512], xb.ap[:, half * 512:(half + 1) * 512], PS[bank][:, :], ALU.add,
                       reads=[xb.res, PSr[bank]], writes=[xb.res])
                yield

            def run_sample(blk, bs, tb, soft_deps):
                ti = blk[0]
                xb = X2b[slots[ti]]
                tok = slice(0, P)
                tsc = bs["TSC"][0]
                QT, KT, KH, MQ, MK = bs["QT"], bs["KT"], bs["KH"], bs["MQ"], bs["MK"]
                QTr, KTr, KHr, MQr, MKr = bs["QTr"], bs["KTr"], bs["KHr"], bs["MQr"], bs["MKr"]
                ELAST, ELr, FEB = bs["ELAST"], bs["ELr"], bs["FEB"]
                MASK = CF[:, CF_MASK8:CF_MASK8 + 128]
                vhg, sgt, kpp, vml, so = tb["vhg"], tb["sgt"], tb["kpp"], tb["vml"], tb["so"]
                tb_ = PBIG.get()
                ptk = psb(tb_, BF16)[:, 0:512].rearrange("p (h k) -> p h k", k=128)
                for h in range(4):
                    tr(ptk[:, h, :], KH.ap[:, h, tok], IDB, reads=[KHr[h], CBr], writes=[PSr[tb_]])
                cp("act", KHTOK.ap, ptk, reads=[PSr[tb_]], writes=[KHTOK.res])
                pa = PA.get()
                pa3 = PS[pa][:, :].rearrange("p (h t) -> p h t", t=128)
                for h in range(4):
                    mm(pa3[:, h, :], KT.ap[:, h, tok], QT.ap[:, h, tok], True, True, reads=[KTr[h], QTr[h]], writes=[PSr[pa]])
                atb = ATB.get()
                tt("dve", atb.ap, pa3, MASK.unsqueeze(1).to_broadcast([P, 4, 128]), ALU.mult, reads=[PSr[pa], CFr], writes=[atb.res])
                po3 = PS[PO_H][:, :].rearrange("p (h v) -> p h v", v=128)
                for h in range(4):
                    mm(po3[:, h, :], atb.ap[:, h, :], vhg.ap[:, h * 128:(h + 1) * 128], h == 0, False,
                       reads=[atb.res, vhg.res], writes=[PSr[PO_H]])

                def state_loop(src_state, dst_state, QX, QXr, kmat, kres, vmat, po, po_res, dec, extra=None):
                    if soft_deps:
                        S.fence_deps = list(S.fence_deps) + soft_deps.pop()
                    QM.i = 0
                    for qb in QM.bufs:
                        memset("pool", qb.ap, 0.0, writes=[qb.res])
                    PRE = 2
                    prepd = {}

                    def prep(j):
                        ssf = SSF.get()
                        dma("sp", ssf.ap, src_state[j].rearrange("h k v -> k h v"), writes=[ssf.res])
                        ssb = SSB.get()
                        cp("act", ssb.ap, ssf.ap, reads=[ssf.res], writes=[ssb.res])
                        qm = QM.get()
                        if j >= len(QM.bufs):
                            jo = j - len(QM.bufs)
                            memset("pool", qm.ap[:, :, jo * 8:(jo + 1) * 8], 0.0, writes=[qm.res])
                        cp("dve", qm.ap[:, :, j * 8:(j + 1) * 8], QX.ap[:, :, j * 8:(j + 1) * 8], reads=QXr, writes=[qm.res])
                        km = KM.get()
                        ts("pool", km.ap, kmat.ap, CF[:, CF_ROWMASK + j:CF_ROWMASK + j + 1], None, ALU.mult, None,
                           reads=[kres, CFr], writes=[km.res])
                        prepd[j] = (ssf, ssb, qm, km)
                    for j in range(PRE):
                        prep(j)
                    for j in range(16):
                        ssf, ssb, qm, km = prepd.pop(j)
                        for h in range(4):
                            mm(po[:, h, :], qm.ap[:, h, :], ssb.ap[:, h, :], False, j == 15,
                               reads=[qm.res, ssb.res], writes=[po_res])
                        bank = PBIG.get()
                        pu = PS[bank][:, :].rearrange("p (h v) -> p h v", v=128)
                        for h in range(4):
                            mm(pu[:, h, :], km.ap[:, h, :], vmat.ap[:, h * 128:(h + 1) * 128], h == 0, h == 3,
                               reads=[km.res, vmat.res], writes=[PSr[bank]])
                        if extra is not None:
                            extra(j, km, 0)
                        if j + PRE < 16:
                            prep(j + PRE)
                        ssn = SSN.get()
                        for h in range(4):
                            sc_ap, sc_res = dec(j, h)
                            stt(ssn.ap[:, h, :], ssf.ap[:, h, :], sc_ap, pu[:, h, :], ALU.mult, ALU.add,
                                reads=[ssf.res, sc_res, PSr[bank]], writes=[ssn.res])
                        odma(dst_state[j].rearrange("h k v -> k h v"), ssn.ap, reads=[ssn.res])
                        if extra is not None:
                            extra(j, km, 1)

                state_loop(sS, oS_s, QT, QTr, KHTOK, KHTOK.res, vhg, po3, PSr[PO_H],
                           lambda j, h: (ELAST.ap[:, h, j:j + 1], ELr[h]))
                sq = SSQ.get()
                for h in range(4):
                    act(JUNK.ap[:, 0:128], po3[:, h, :], AF.Square, reads=[PSr[PO_H]], writes=[JUNK.res, sq.res],
                        accum_out=sq.ap[:, h:h + 1])
                ts("dve", sq.ap[:, 4:8], sq.ap[:, 0:4], 1.0 / 128, EPS, ALU.mult, ALU.add, reads=[sq.res], writes=[sq.res])
                tt("pool", sq.ap[:, 4:8], sq.ap[:, 4:8], CF[:, CF_NHALF:CF_NHALF + 4], ALU.pow, reads=[sq.res, CFr], writes=[sq.res])
                for h in range(4):
                    stt(OCAT.ap[:, h * 128:(h + 1) * 128], po3[:, h, :], sq.ap[:, 4 + h:5 + h], sgt.ap[:, h * 128:(h + 1) * 128],
                        ALU.mult, ALU.mult, reads=[PSr[PO_H], sq.res, sgt.res], writes=[OCAT.res])
                pa = PA.get()
                pa3 = PS[pa][:, :].rearrange("p (h t) -> p h t", t=128)
                for h in range(4):
                    mm(pa3[:, h, :], MK.ap[:, h, tok], MQ.ap[:, h, tok], True, True, reads=[MKr[h], MQr[h]], writes=[PSr[pa]])
                atb = ATB.get()
                for h in range(4):
                    stt(atb.ap[:, h, :], pa3[:, h, :], tsc.ap[:, h:h + 1], MASK, ALU.mult, ALU.mult,
                        reads=[PSr[pa], tsc.res, CFr], writes=[atb.res])
                pn3 = PS[PO_M][:, :].rearrange("p (h v) -> p h v", v=128)
                dma("sp", SN_ROW.ap[0:64, :], sn[:, :], writes=[SN_ROW.res])
                psn = PS[PTR][:, 340:404]
                tr(psn, SN_ROW.ap[0:64, :], CF[0:64, CF_IDENT:CF_IDENT + 64], reads=[SN_ROW.res, CFr], writes=[PSr[PTR]])
                cp("dve", SN_F.ap, psn, reads=[PSr[PTR]], writes=[SN_F.res])
                cp("dve", SN_B.ap, SN_F.ap, reads=[SN_F.res], writes=[SN_B.res])
                snf3 = SN_F.ap.rearrange("p (j h) -> p j h", h=4)
                snb3 = SN_B.ap.rearrange("p (j h) -> p j h", h=4)
                snn3 = SN_NEW.ap.rearrange("p (j h) -> p j h", h=4)
                for h in range(4):
                    mm(pn3[:, h, :], atb.ap[:, h, :], vml.ap[:, h * 128:(h + 1) * 128], h == 0, False,
                       reads=[atb.res, vml.res], writes=[PSr[PO_M]])

                def n_update(j, km, part):
                    pnn = PS[PTR][:, 272:276]
                    if part == 0:
                        for h in range(4):
                            mm(pnn[:, h:h + 1], km.ap[:, h, :], ONEC.ap[:, 0:1], True, True, reads=[km.res, ONEC.res], writes=[PSr[PTR]])
                    else:
                        tt("dve", snn3[:, j, :], snf3[:, j, :], FEB.ap[:, :, j], ALU.mult, reads=[SN_F.res, FEB.res], writes=[SN_NEW.res])
                        tt("dve", snn3[:, j, :], snn3[:, j, :], pnn, ALU.add, reads=[SN_NEW.res, PSr[PTR]], writes=[SN_NEW.res])
                state_loop(sC, oC_s, MQ, MQr, kpp, kpp.res, vml, pn3, PSr[PO_M],
                           lambda j, h: (FEB.ap[:, h, j:j + 1], FEB.res), extra=n_update)
                pqn = PS[PTR][:, 404:468].rearrange("p (h j) -> p h j", j=16)
                for h in range(4):
                    mm(pqn[:, h, :], MQ.ap[:, h, tok], snb3[:, :, h], True, True, reads=[MQr[h], SN_B.res], writes=[PSr[PTR]])
                qn = SSQ.get()
                qn3 = SN_QN.ap.rearrange("p (h j) -> p h j", j=16)
                tt("dve", qn3, pqn, CF[:, CF_ROWMASK:CF_ROWMASK + 16].unsqueeze(1).to_broadcast([P, 4, 16]), ALU.mult,
                   reads=[PSr[PTR], CFr], writes=[SN_QN.res])
                treduce(qn.ap[:, 0:4], qn3, reads=[SN_QN.res], writes=[qn.res])
                pd = PS[PTR][:, 268:272]
                for h in range(4):
                    mm(pd[:, h:h + 1], atb.ap[:, h, :], ONEC.ap[:, 0:1], True, True, reads=[atb.res, ONEC.res], writes=[PSr[PTR]])
                pd_sb = SSQ.get()
                tt("dve", pd_sb.ap[:, 0:4], pd, qn.ap[:, 0:4], ALU.add, reads=[PSr[PTR], qn.res], writes=[pd_sb.res])
                pb = PBIG.get()
                tr(PS[pb][0:64, 0:128], SN_NEW.ap, IDF, reads=[SN_NEW.res, CFr], writes=[PSr[pb]])
                cp("dve", SN_ROW.ap[0:64, :], PS[pb][0:64, 0:128], reads=[PSr[pb]], writes=[SN_ROW.res])
                odma(on_s[:, :], SN_ROW.ap[0:64, :], reads=[SN_ROW.res])
                ml_norm(pn3, pd_sb.ap[:, 0:4], pd_sb.res, tsc, so)
                for _ in out_proj(xb):
                    pass

            pblocks = [b for b in blocks if tiles[b[0]] != SAMPLE]
            sblocks = [b for b in blocks if tiles[b[0]] == SAMPLE]
            assert len(sblocks) <= 1 and pblocks
            import itertools
            flat = [(k, bi) for k, b in enumerate(pblocks) for bi in range(len(b))]
            tbs = [dict() for _ in flat]
            tb_s = {}
            ks = len(pblocks)

            allblocks = [(pblocks[k], sets[k % 2], "p%d" % k) for k in range(len(pblocks))]
            if sblocks:
                allblocks.append((sblocks[0], sets[ks % 2], "s"))

            def fillers_for(n):
                gens = []
                if n == len(flat):
                    sb_, bs_, key = allblocks[ks]
                    return itertools.chain(lab(gen_At(sb_, bs_, key), "Ats"), lab(gen_B1(sb_, bs_), "B1s"), lab(gen_B2(sb_, bs_, 0, tb_s), "B2s"))
                k, bi = flat[n]
                blk_, bs_, key = allblocks[k]
                if bi == 0:
                    gens.append(lab(gen_At(blk_, bs_, key), "At%d" % k))
                    gens.append(lab(gen_B1(blk_, bs_), "B1_%d" % k))
                gens.append(lab(gen_B2(blk_, bs_, bi, tbs[n]), "B2_%d" % n))
                if bi == len(blk_) - 1 and k + 1 < len(allblocks):
                    nb_, nbs_, nkey = allblocks[k + 1]
                    gens.append(lab(gen_Ac(nb_, nbs_, nkey), "Ac%d" % (k + 1)))
                return itertools.chain(*gens)

            def count_units(n):
                if n == len(flat):
                    return 5 + 2 + 19
                k, bi = flat[n]
                c = 5
                if bi == 0:
                    c += 2 * len(pblocks[k]) + 19
                if bi == len(pblocks[k]) - 1 and k + 1 < len(allblocks):
                    c += len(allblocks[k + 1][0])
                return c

            ntot = len(flat) + (1 if sblocks else 0)
            for _ in lab(gen_Ac(allblocks[0][0], allblocks[0][1], allblocks[0][2]), "Ac0"):
                pass
            f0 = fillers_for(0)
            for _ in f0:
                pass
            if first:
                load_wout()
            for n, (k, bi) in enumerate(flat):
                main = lab(gen_R(pblocks[k], sets[k % 2], bi, tbs[n]), "R%d" % n)
                if n + 1 < ntot:
                    fill = fillers_for(n + 1)
                    nun = count_units(n + 1)
                else:
                    fill = iter(())
                    nun = 0
                nsteps = 13
                per = -(-nun // nsteps) if nun else 0
                for step in main:
                    for _ in range(per):
                        try:
                            next(fill)
                        except StopIteration:
                            break
                for _ in fill:
                    pass
            if sblocks:
                deps = []
                for e in ENGS:
                    for op in reversed(S.streams[e]):
                        if not op.is_dma:
                            deps.append(op)
                            break
                deps.extend(S.dmas_since_fence)
                S.tag = "SAMPLE"
                run_sample(sblocks[0], sets[ks % 2], tb_s, [deps])

        def phase2(tiles, slots, last):
            cv = Carver()
            S.tag = "P2"
            ntl = len(tiles)
            NTOK = ntl * P
            H2T = cv.take([NKC, NTOK], BF16, "h2T")
            H2r = [Res("h2t%d" % i) for i in range(ntl)]
            HBF = [cv.take([D], BF16, "hbf%d" % i) for i in range(2)]
            JUNK = HBF[0]
            XSS = Ring([cv.take([2], F32, "xss%d" % i) for i in range(4)])
            WG = Ring([cv.take([NKC, GFF * P], BF16, "wg%d" % i) for i in range(2)])
            WV = Ring([cv.take([NKC, GFF * P], BF16, "wv%d" % i) for i in range(2)])
            WD = Ring([cv.take([GFF, D], BF16, "wd%d" % i) for i in range(2)])
            GT = Ring([cv.take([GFF, NTOK], BF16, "gT%d" % i) for i in range(2)])
            UB = Ring([cv.take([2 + 512], F32, "ub%d" % i) for i in range(2)])
            ACC = Ring([cv.take([512], F32, "acc%d" % i) for i in range(2)])
            CST = cv.take([NFC, 32], F32, "cst")
            CROW = Ring([cv.take([128], F32, "crow%d" % i, parts=32) for i in range(3)])
            CTL = cv.take([NFC, 32], F32, "ctl")
            has_s = SAMPLE in tiles
            blocks = []
            i = 0
            while i < ntl:
                if tiles[i] == SAMPLE:
                    blocks.append([i])
                    i += 1
                else:
                    blocks.append(list(range(i, min(i + 4, ntl))))
                    i += len(blocks[-1])

            last_prompt_b = max(k for k, blk in enumerate(blocks) if tiles[blk[0]] != SAMPLE)
            PU = Ring([0, 1])
            PV = Ring([2, 3])
            PD2 = Ring([4, 5, 6, 7])
            PTR = 0

            dma("sp", GB.ap, g2[0, :].partition_broadcast(P), writes=[GB.res])
            def n2_chain(ti):
                xb = X2b[slots[ti]]
                ss = XSS.get()
                hb = HBF[ti % 2]
                act(hb.ap, xb.ap, AF.Square, reads=[xb.res], writes=[hb.res, ss.res], accum_out=ss.ap[:, 0:1])
                ts("dve", ss.ap[:, 1:2], ss.ap[:, 0:1], 1.0 / D, EPS, ALU.mult, ALU.add, reads=[ss.res], writes=[ss.res])
                tt("pool", ss.ap[:, 1:2], ss.ap[:, 1:2], CF[:, CF_NHALF:CF_NHALF + 1], ALU.pow, reads=[ss.res, CFr], writes=[ss.res])
                stt(hb.ap, xb.ap, ss.ap[:, 1:2], GB.ap, ALU.mult, ALU.mult, reads=[xb.res, ss.res, GB.res], writes=[hb.res])
            n2_chain(0)
            for ti in range(ntl):
                if ti + 1 < ntl:
                    n2_chain(ti + 1)
                hb = HBF[ti % 2]
                bank = PD2.get()
                pt = psb(bank, BF16).rearrange("p (k t) -> p k t", t=128)
                for kc in range(NKC):
                    tr(pt[:, kc, :], hb.ap[:, kc * P:(kc + 1) * P], IDB, reads=[hb.res, CBr], writes=[PSr[bank]])
                cp("act", H2T.ap[:, :, ti * P:(ti + 1) * P], pt, reads=[PSr[bank]], writes=[H2r[ti]])
            def cst_prep(c):
                cr = CROW.get()
                dma("sp", cr.ap, sconv[:, c * P:(c + 1) * P], writes=[cr.res])
                bank = PD2.get()
                tr(PS[bank][:, 0:32], cr.ap, CF[0:32, CF_IDENT:CF_IDENT + 32], reads=[cr.res, CFr], writes=[PSr[bank]])
                cp("dve", CST.ap[:, c, :], PS[bank][:, 0:32], reads=[PSr[bank]], writes=[CST.res])
            if has_s:
                cst_prep(0)

            wg_r = w_gate.rearrange("(kc p) f -> p kc f", p=P)
            wv_r = w_val.rearrange("(kc p) f -> p kc f", p=P)
            wd_r = w_down.rearrange("(c p) d -> p c d", p=P)
            ngroups = NFC // GFF
            for g in range(ngroups):
                wg = WG.get()
                wv = WV.get()
                wd = WD.get()
                f0 = g * GFF * P
                dma("pool", wg.ap, wg_r[:, :, f0:f0 + GFF * P], writes=[wg.res])
                dma("pool", wv.ap, wv_r[:, :, f0:f0 + GFF * P], writes=[wv.res])
                dma("pool", wd.ap, wd_r[:, g * GFF:(g + 1) * GFF, :], writes=[wd.res])
                gt = GT.get()
                gtr = [[Res("gt_%d_%d" % (cc, b)) for b in range(len(blocks))] for cc in range(GFF)]
                for cc in range(GFF):
                    c = g * GFF + cc
                    prev_ub = None
                    if has_s and c + 1 < NFC:
                        cst_prep(c + 1)
                    for b_i, blk in enumerate(blocks):
                        is_s = tiles[blk[0]] == SAMPLE
                        NT = len(blk) * P
                        t0 = blk[0] * P
                        h2r = [H2r[ti] for ti in blk]
                        bu = PU.get()
                        for kc in range(NKC):
                            mm(PS[bu][:, 0:NT], wg.ap[:, kc, cc * P:(cc + 1) * P], H2T.ap[:, kc, t0:t0 + NT], kc == 0, kc == NKC - 1,
                               reads=[wg.res] + h2r, writes=[PSr[bu]])
                        bv = PV.get()
                        for kc in range(NKC):
                            mm(PS[bv][:, 0:NT], wv.ap[:, kc, cc * P:(cc + 1) * P], H2T.ap[:, kc, t0:t0 + NT], kc == 0, kc == NKC - 1,
                               reads=[wv.res] + h2r, writes=[PSr[bv]])
                        ub = UB.get()
                        acc = ACC.get()
                        if is_s:
                            u3 = ub.ap[:, 0:160].rearrange("p (j t) -> p j t", t=10)
                            cp("pool", u3[:, :, 0:2], CST.ap[:, c, :].rearrange("p (j t) -> p j t", t=2), reads=[CST.res], writes=[ub.res])
                            cp("act", u3[:, :, 2:10], PS[bu][:, 0:128].rearrange("p (j t) -> p j t", t=8), reads=[PSr[bu]], writes=[ub.res])
                            a3 = acc.ap[:, 0:128].rearrange("p (j t) -> p j t", t=8)
                            ts("dve", a3, u3[:, :, 2:10], CW[:, c, 2:3], CBIAS[:, c:c + 1], ALU.mult, ALU.add, reads=[ub.res, SMr], writes=[acc.res])
                            stt(a3, u3[:, :, 1:9], CW[:, c, 1:2], a3, ALU.mult, ALU.add, reads=[ub.res, SMr, acc.res], writes=[acc.res])
                            stt(a3, u3[:, :, 0:8], CW[:, c, 0:1], a3, ALU.mult, ALU.add, reads=[ub.res, SMr, acc.res], writes=[acc.res])
                            cp("pool", CTL.ap[:, c, :].rearrange("p (j t) -> p j t", t=2), u3[:, :, 8:10], reads=[ub.res], writes=[CTL.res])
                        else:
                            first_blk = (tiles[blk[0]] == 0)
                            if first_blk or prev_ub is None:
                                if first_blk:
                                    memset("pool", ub.ap[:, 0:2], 0.0, writes=[ub.res])
                                else:
                                    cp("pool", ub.ap[:, 0:2], HALO.ap[:, c, :], reads=[HALO.res], writes=[ub.res])
                            else:
                                cp("pool", ub.ap[:, 0:2], prev_ub.ap[:, 512:514], reads=[prev_ub.res], writes=[ub.res])
                            cp("act", ub.ap[:, 2:2 + NT], PS[bu][:, 0:NT], reads=[PSr[bu]], writes=[ub.res])
                            ts("pool", acc.ap[:, 0:NT], ub.ap[:, 2:2 + NT], CW[:, c, 2:3], CBIAS[:, c:c + 1], ALU.mult, ALU.add,
                               reads=[ub.res, SMr], writes=[acc.res])
                            stt(acc.ap[:, 0:NT], ub.ap[:, 1:1 + NT], CW[:, c, 1:2], acc.ap[:, 0:NT], ALU.mult, ALU.add, reads=[ub.res, SMr, acc.res], writes=[acc.res])
                            stt(acc.ap[:, 0:NT], ub.ap[:, 0:NT], CW[:, c, 0:1], acc.ap[:, 0:NT], ALU.mult, ALU.add, reads=[ub.res, SMr, acc.res], writes=[acc.res])
                            prev_ub = ub
                            last_blk = (tiles[blk[-1]] == NPT - 1)
                            if last_blk:
                                cp("pool", CTL.ap[:, c, 0:2], ub.ap[:, NT:NT + 2], reads=[ub.res], writes=[CTL.res])
                            elif b_i == last_prompt_b:
                                cp("pool", HALO.ap[:, c, :], ub.ap[:, NT:NT + 2], reads=[ub.res], writes=[HALO.res])
                        act(acc.ap[:, 0:NT], acc.ap[:, 0:NT], AF.Gelu_apprx_tanh, reads=[acc.res], writes=[acc.res])
                        tt("dve", gt.ap[:, cc, t0:t0 + NT], acc.ap[:, 0:NT], PS[bv][:, 0:NT], ALU.mult, reads=[acc.res, PSr[bv]], writes=[gtr[cc][b_i]])
                for ti in range(ntl):
                    xb = X2b[slots[ti]]
                    b_i = [k for k, blk in enumerate(blocks) if ti in blk][0]
                    for half in range(2):
                        bank = PD2.get()
                        for cc in range(GFF):
                            mm(PS[bank][:, :], gt.ap[:, cc, ti * P:(ti + 1) * P], wd.ap[:, cc, half * 512:(half + 1) * 512], cc == 0, cc == GFF - 1,
                               reads=[gtr[cc][b_i], wd.res], writes=[PSr[bank]])
                        tt("dve", xb.ap[:, half * 512:(half + 1) * 512], xb.ap[:, half * 512:(half + 1) * 512], PS[bank][:, :], ALU.add,
                           reads=[xb.res, PSr[bank]], writes=[xb.res])
            if has_s:
                for c in range(NFC):
                    bank = PD2.get()
                    tr(PS[bank][0:32, 0:128], CTL.ap[:, c, :], IDF, reads=[CTL.res, CFr], writes=[PSr[bank]])
                    cr = CROW.get()
                    cp("dve", cr.ap, PS[bank][0:32, 0:128], reads=[PSr[bank]], writes=[cr.res])
                    odma(ocv_s[:, c * P:(c + 1) * P], cr.ap, reads=[cr.res])
            if last:
                for c in range(NFC):
                    bank = PD2.get()
                    tr(PS[bank][0:2, 0:128], CTL.ap[:, c, 0:2], IDF, reads=[CTL.res, CFr], writes=[PSr[bank]])
                    cr = CROW.get()
                    cp("dve", cr.ap[0:2, :], PS[bank][0:2, 0:128], reads=[PSr[bank]], writes=[cr.res])
                    odma(ocv_p[:, c * P:(c + 1) * P], cr.ap[0:2, :], reads=[cr.res])
            dma("sp", GB.ap, gf[0, :].partition_broadcast(P), writes=[GB.res])
            fss = {}

            def fn_stats(ti):
                xb = X2b[slots[ti]]
                ss = XSS.get()
                act(JUNK.ap, xb.ap, AF.Square, reads=[xb.res], writes=[JUNK.res, ss.res], accum_out=ss.ap[:, 0:1])
                ts("dve", ss.ap[:, 1:2], ss.ap[:, 0:1], 1.0 / D, EPS, ALU.mult, ALU.add, reads=[ss.res], writes=[ss.res])
                tt("pool", ss.ap[:, 1:2], ss.ap[:, 1:2], CF[:, CF_NHALF:CF_NHALF + 1], ALU.pow, reads=[ss.res, CFr], writes=[ss.res])
                fss[ti] = ss
            fn_stats(0)
            for ti in range(ntl):
                if ti + 1 < ntl:
                    fn_stats(ti + 1)
                tile = tiles[ti]
                xb = X2b[slots[ti]]
                ss = fss.pop(ti)
                stt(xb.ap, xb.ap, ss.ap[:, 1:2], GB.ap, ALU.mult, ALU.mult, reads=[xb.res, ss.res, GB.res], writes=[xb.res])
                dst = y_s[:, :] if tile == SAMPLE else y_p[tile * P:(tile + 1) * P, :]
                odma(dst, xb.ap, reads=[xb.res], nofence=True)


        for si, tl in enumerate(SBS):
            slots = list(range(len(tl)))
            phase1(tl, slots, si == 0)
            if si == len(SBS) - 1:
                odma(oS_p.rearrange("h k v -> k h v"), SF.ap, reads=[SF.res])
                odma(oC_p.rearrange("h k v -> k h v"), CFs.ap, reads=[CFs.res])
                odma(om_p[:, :], MOUT.ap, reads=[MOUT.res])
            S.fence()
            phase2(tl, slots, si == len(SBS) - 1)
            S.fence()
        tr(PS[0][0:4, 0:128], NF.ap, IDF, reads=[NF.res, CFr], writes=[PSr[0]])
        cp("dve", NROW.ap, PS[0][0:4, 0:128], reads=[PSr[0]], writes=[NROW.res])
        odma(on_p[:, :], NROW.ap, reads=[NROW.res])
        S.out_dmas = out_dmas
        S.emit()
    return nc


_NC_CACHE = {}


def _get_nc():
    if "nc" not in _NC_CACHE:
        _NC_CACHE["nc"] = build()
    return _NC_CACHE["nc"]


def make_in_maps(inputs, cores):
    f = lambda a: np.ascontiguousarray(np.asarray(a, dtype=np.float32))
    x_prompt = f(inputs["x_prompt"]); x_sample = f(inputs["x_sample"])
    sS = f(inputs["state_hgrn_S"])[0]; sC = f(inputs["state_mlstm_C"])[0]
    sn = f(inputs["state_mlstm_n"])[0]; sm = f(inputs["state_mlstm_m"])[0]
    sconv = f(inputs["state_conv"])[0]
    cf, cb = _consts()
    lbl = f(inputs["hg_lb_logits"])
    lbl_fm = np.ascontiguousarray(lbl.reshape(2, 4, 128).transpose(2, 0, 1).reshape(128, 8))
    gcat = np.concatenate([f(inputs["hg_norm_g"])[0], f(inputs["ml_norm_g"])[0]])
    gcat_fm = np.ascontiguousarray(gcat.reshape(8, 128).T)
    bigfg = np.ascontiguousarray(np.stack([f(inputs["ml_b_ig"])[0], f(inputs["ml_b_fg"])[0]], axis=1))
    cw = f(inputs["conv_w"])[0]
    cw_fm = np.ascontiguousarray(cw.reshape(3, NFC, 128).transpose(2, 1, 0).reshape(128, NFC * 3))
    cbias_fm = np.ascontiguousarray(f(inputs["conv_b"])[0].reshape(NFC, 128).T)
    shared = {
        "w_in": f(inputs["w_in"])[0], "w_out": f(inputs["w_out"])[0], "w_gate": f(inputs["w_gate"])[0],
        "w_val": f(inputs["w_val"])[0], "w_down": f(inputs["w_down"])[0],
        "g1": f(inputs["norm1_g"]), "g2": f(inputs["norm2_g"]), "gf": f(inputs["final_norm_g"]).reshape(1, D),
        "lbl": lbl_fm, "gcat": gcat_fm, "bigfg": bigfg, "cw": cw_fm, "cbias": cbias_fm, "cf": cf, "cb": cb,
    }
    maps = []
    for c in cores:
        sl = slice(c * 16, (c + 1) * 16)
        m = dict(shared)
        m["xp"] = x_prompt[c]
        m["xs"] = np.ascontiguousarray(x_sample[sl].reshape(128, D))
        m["sS"] = np.ascontiguousarray(sS[sl])
        m["sC"] = np.ascontiguousarray(sC[sl])
        m["sn"] = np.ascontiguousarray(sn[sl].reshape(64, 128))
        m["sm"] = np.ascontiguousarray(sm[sl].T)
        m["sconv"] = np.ascontiguousarray(sconv[sl].reshape(32, DFF))
        maps.append(m)
    return maps


def assemble(results):
    n = len(results)
    y_p = np.stack([r["y_p"] for r in results])
    y_s = np.concatenate([r["y_s"].reshape(16, 8, D) for r in results])
    S_p = np.stack([r["oS_p"] for r in results])[None]
    S_s = np.concatenate([r["oS_s"] for r in results])[None]
    C_p = np.stack([r["oC_p"] for r in results])[None]
    C_s = np.concatenate([r["oC_s"] for r in results])[None]
    n_p = np.stack([r["on_p"] for r in results])[None]
    n_s = np.concatenate([r["on_s"].reshape(16, 4, 128) for r in results])[None]
    m_p = np.stack([r["om_p"].reshape(4) for r in results])[None]
    m_s = np.concatenate([r["om_s"].T for r in results])[None]
    cv_p = np.stack([r["ocv_p"] for r in results])[None]
    cv_s = np.concatenate([r["ocv_s"].reshape(16, 2, DFF) for r in results])[None]
    outs = (y_p, y_s, S_p, S_s, C_p, C_s, n_p, n_s, m_p, m_s, cv_p, cv_s)
    return tuple(np.ascontiguousarray(o, dtype=np.float32) for o in outs)


def kernel(**inputs):
    nc = _get_nc()
    maps = make_in_maps(inputs, list(range(NCORES)))
    res = run_bass_kernel_spmd(nc, maps, core_ids=list(range(NCORES)))
    return assemble(res.results)
```
